# Optimizing a Trainium2 kernel written in Bass

```python
import math
import jax
import jax.numpy as jnp
from jax import lax
import numpy as np

D_MODEL = 1024
BATCH = 8
SEQ = 4096
DEPTH = 2

GRID_W = 64
CTX_LEN = 256
N_EVEN = (DEPTH + 1) // 2
N_ODD = DEPTH // 2
N_MOD = 9
EPS = 1e-6
D_FF = 256 * math.ceil(8 * D_MODEL / 3 / 256)

HEAD_DIM = 64
ATTN_WIDTH = D_MODEL // 2
N_Q_HEADS = ATTN_WIDTH // HEAD_DIM
N_KV_HEADS = max(1, N_Q_HEADS // 4)
KV_GROUP = N_Q_HEADS // N_KV_HEADS
Q_WIDTH = N_Q_HEADS * HEAD_DIM
KV_WIDTH = N_KV_HEADS * HEAD_DIM
WINDOW = 128
BLOCK = 128
ROPE_BASE = 10000.0
NEG_INF = -1e30

SSM_WIDTH = D_MODEL - ATTN_WIDTH
SSM_GROUP = 16
SSM_GROUPS = SSM_WIDTH // SSM_GROUP
SSM_STATE = 64
DT_MIN = 1e-3
DT_MAX = 1e-1

IN_WIDTH = Q_WIDTH + 2 * KV_WIDTH + SSM_WIDTH

FOURIER_GROUPS = 4
FOURIER_GROUP_WIDTH = D_MODEL // FOURIER_GROUPS

kernel_name = "hybrid_swa_s5_fnet_macaron_prefix"


def rms_norm(x, gain):
    xf = x.astype(jnp.float32)
    y = xf * lax.rsqrt(jnp.mean(xf * xf, axis=-1, keepdims=True) + EPS)
    return (y * gain.astype(jnp.float32)).astype(x.dtype)


def modulate(h, shift, scale):
    return h * (1.0 + scale) + shift


def ada_mod(cond, w, b):
    m = (jax.nn.silu(cond) @ w + b)[..., None, :]
    return jnp.split(m, N_MOD, axis=-1)


def swiglu(h, w1, w3, w2):
    return (jax.nn.silu(h @ w1) * (h @ w3)) @ w2


def half_ffn(s, gain, mod3, w1, w3, w2):
    shift, scale, gate = mod3
    return s + 0.5 * gate * swiglu(modulate(rms_norm(s, gain), shift, scale), w1, w3, w2)


def axial_rope_tables(rows):
    row = jnp.repeat(jnp.arange(rows, dtype=jnp.float32), GRID_W)
    col = jnp.tile(jnp.arange(GRID_W, dtype=jnp.float32), rows)
    n_freq = HEAD_DIM // 4
    inv_freq = 1.0 / (ROPE_BASE ** (jnp.arange(n_freq, dtype=jnp.float32) / n_freq))
    ang = jnp.concatenate([row[:, None] * inv_freq, col[:, None] * inv_freq], axis=-1)
    return jnp.cos(ang), jnp.sin(ang)


def apply_rope(x, cos, sin):
    x1, x2 = jnp.split(x.astype(jnp.float32), 2, axis=-1)
    c = cos[None, :, None, :]
    s = sin[None, :, None, :]
    return jnp.concatenate([x1 * c - x2 * s, x1 * s + x2 * c], axis=-1).astype(x.dtype)


def project_heads(h, w_in, q_gain, k_gain, with_q):
    b, n = h.shape[:2]
    if with_q:
        q_part, rest = jnp.split(h @ w_in, [Q_WIDTH], axis=-1)
        q = rms_norm(q_part.reshape(b, n, N_Q_HEADS, HEAD_DIM), q_gain)
    else:
        rest = h @ w_in[:, Q_WIDTH:]
        q = None
    k, v, u = jnp.split(rest, [KV_WIDTH, 2 * KV_WIDTH], axis=-1)
    k = rms_norm(k.reshape(b, n, N_KV_HEADS, HEAD_DIM), k_gain)
    v = v.reshape(b, n, N_KV_HEADS, HEAD_DIM)
    u = u.reshape(b, n, SSM_GROUPS, SSM_GROUP)
    return q, k, v, u


def window_attention(q, k, v, kc, vc, sink):
    b, n = q.shape[:2]
    nb = n // BLOCK
    span = BLOCK + 2 * WINDOW
    scale = HEAD_DIM ** -0.5
    qb = q.reshape(b, nb, BLOCK, N_KV_HEADS, KV_GROUP, HEAD_DIM)
    starts = jnp.arange(nb) * BLOCK
    idx = starts[:, None] + jnp.arange(span)[None, :]
    pad = ((0, 0), (WINDOW, WINDOW), (0, 0), (0, 0))
    kb = jnp.take(jnp.pad(k, pad), idx, axis=1)
    vb = jnp.take(jnp.pad(v, pad), idx, axis=1)
    s_win = jnp.einsum('bnqhgd,bnkhd->bnhgqk', qb, kb).astype(jnp.float32) * scale
    s_ctx = jnp.einsum('bnqhgd,bchd->bnhgqc', qb, kc).astype(jnp.float32) * scale
    key_pos = idx - WINDOW
    q_pos = starts[:, None] + jnp.arange(BLOCK)[None, :]
    ok = ((jnp.abs(key_pos[:, None, :] - q_pos[:, :, None]) <= WINDOW)
          & (key_pos[:, None, :] >= 0) & (key_pos[:, None, :] < n))
    s_win = jnp.where(ok[None, :, None, None], s_win, NEG_INF)
    sink_col = jnp.broadcast_to(
        sink.astype(jnp.float32).reshape(1, 1, N_KV_HEADS, KV_GROUP, 1, 1), s_win.shape[:-1] + (1,))
    p = jax.nn.softmax(jnp.concatenate([s_win, s_ctx, sink_col], axis=-1), axis=-1)
    n_ctx = kc.shape[1]
    p_win = p[..., :span].astype(v.dtype)
    p_ctx = p[..., span:span + n_ctx].astype(v.dtype)
    o = (jnp.einsum('bnhgqk,bnkhd->bnqhgd', p_win, vb)
         + jnp.einsum('bnhgqc,bchd->bnqhgd', p_ctx, vc))
    return o.reshape(b, n, Q_WIDTH)


def context_attention(qc, kc, vc, sink):
    b, n = qc.shape[:2]
    qg = qc.reshape(b, n, N_KV_HEADS, KV_GROUP, HEAD_DIM)
    s = jnp.einsum('bqhgd,bkhd->bhgqk', qg, kc).astype(jnp.float32) * (HEAD_DIM ** -0.5)
    sink_col = jnp.broadcast_to(
        sink.astype(jnp.float32).reshape(1, N_KV_HEADS, KV_GROUP, 1, 1), s.shape[:-1] + (1,))
    p = jax.nn.softmax(jnp.concatenate([s, sink_col], axis=-1), axis=-1)[..., :-1]
    o = jnp.einsum('bhgqk,bkhd->bqhgd', p.astype(vc.dtype), vc)
    return o.reshape(b, n, Q_WIDTH)


def s5_discretize(lam_re, lam_im, log_dt, b_re, b_im):
    dt = jnp.exp(log_dt.astype(jnp.float32))[:, None]
    lr = lam_re.astype(jnp.float32)
    li = lam_im.astype(jnp.float32)
    mag = jnp.exp(lr * dt)
    ar = mag * jnp.cos(li * dt)
    ai = mag * jnp.sin(li * dt)
    nr = ar - 1.0
    den = lr * lr + li * li
    fr = (nr * lr + ai * li) / den
    fi = (ai * lr - nr * li) / den
    br = b_re.astype(jnp.float32)
    bi = b_im.astype(jnp.float32)
    bbr = fr[..., None] * br - fi[..., None] * bi
    bbi = fr[..., None] * bi + fi[..., None] * br
    return ar, ai, bbr, bbi


def s5_combine(e1, e2):
    a1r, a1i, b1r, b1i = e1
    a2r, a2i, b2r, b2i = e2
    return (a1r * a2r - a1i * a2i,
            a1r * a2i + a1i * a2r,
            a2r * b1r - a2i * b1i + b2r,
            a2r * b1i + a2i * b1r + b2i)


def s5_direction(u, uc, lam_re, lam_im, log_dt, b_re, b_im, c_re, c_im, reverse, want_ctx):
    ar, ai, bbr, bbi = s5_discretize(lam_re, lam_im, log_dt, b_re, b_im)
    cr = c_re.astype(jnp.float32)
    ci = c_im.astype(jnp.float32)

    def drive(x):
        xf = x.astype(jnp.float32)
        return (jnp.einsum('bngi,gpi->bngp', xf, bbr), jnp.einsum('bngi,gpi->bngp', xf, bbi))

    def scan(xr, xi):
        n = xr.shape[1]
        a_r = jnp.broadcast_to(ar[None, None], (1, n) + ar.shape)
        a_i = jnp.broadcast_to(ai[None, None], (1, n) + ai.shape)
        _, _, hr, hi = lax.associative_scan(s5_combine, (a_r, a_i, xr, xi), reverse=reverse, axis=1)
        return hr, hi

    def readout(hr, hi):
        return jnp.einsum('gjp,bngp->bngj', cr, hr) - jnp.einsum('gjp,bngp->bngj', ci, hi)

    hcr, hci = scan(*drive(uc))
    end = 0 if reverse else -1
    h0r, h0i = hcr[:, end], hci[:, end]
    xr, xi = drive(u)
    first = -1 if reverse else 0
    xr = xr.at[:, first].add(ar * h0r - ai * h0i)
    xi = xi.at[:, first].add(ar * h0i + ai * h0r)
    hr, hi = scan(xr, xi)
    y = readout(hr, hi)
    yc = readout(hcr, hci) if want_ctx else None
    return y, yc


def s5_glu(y, w_glu):
    g = jax.nn.gelu(y.reshape(y.shape[0], y.shape[1], SSM_WIDTH))
    return g * jax.nn.sigmoid(g @ w_glu)


def s5_bidirectional(u, uc, lam_re, lam_im, log_dt, b_re, b_im, c_re, c_im, d_skip, w_glu, want_ctx):
    d = d_skip.astype(jnp.float32).reshape(SSM_GROUPS, SSM_GROUP)
    y = d * u.astype(jnp.float32)
    yc = d * uc.astype(jnp.float32) if want_ctx else None
    for direction, reverse in ((0, False), (1, True)):
        yd, ycd = s5_direction(u, uc, lam_re[direction], lam_im[direction], log_dt[direction],
                               b_re[direction], b_im[direction], c_re[direction], c_im[direction],
                               reverse, want_ctx)
        y = y + yd
        if want_ctx:
            yc = yc + ycd
    out = s5_glu(y, w_glu).astype(u.dtype)
    out_c = s5_glu(yc, w_glu).astype(uc.dtype) if want_ctx else None
    return out, out_c


def attn_ssm_mixer(h, hc, cos, sin, w_in, q_gain, k_gain, sink, lam_re, lam_im, log_dt,
                   b_re, b_im, c_re, c_im, d_skip, w_glu, w_out, want_ctx):
    q, k, v, u = project_heads(h, w_in, q_gain, k_gain, True)
    qc, kc, vc, uc = project_heads(hc, w_in, q_gain, k_gain, want_ctx)
    q = apply_rope(q, cos, sin)
    k = apply_rope(k, cos, sin)
    attn = window_attention(q, k, v, kc, vc, sink)
    ssm, ssm_c = s5_bidirectional(u, uc, lam_re, lam_im, log_dt, b_re, b_im, c_re, c_im,
                                  d_skip, w_glu, want_ctx)
    y = jnp.concatenate([attn, ssm.astype(attn.dtype)], axis=-1) @ w_out
    if not want_ctx:
        return y, None
    attn_c = context_attention(qc, kc, vc, sink)
    yc = jnp.concatenate([attn_c, ssm_c.astype(attn_c.dtype)], axis=-1) @ w_out
    return y, yc


def fourier_mix(h, w):
    b, n, d = h.shape
    hg = h.astype(jnp.float32).reshape(b, n, FOURIER_GROUPS, FOURIER_GROUP_WIDTH)
    f = jnp.fft.fftn(hg, axes=(1, 3), norm="ortho").real
    return f.reshape(b, n, d).astype(h.dtype) @ w


def odd_layer_stream(s, gains, mod, w1, w3, w2, w_f):
    s = half_ffn(s, gains[0], mod[0:3], w1[0], w3[0], w2[0])
    h = modulate(rms_norm(s, gains[1]), mod[3], mod[4])
    s = s + mod[5] * fourier_mix(h, w_f)
    return half_ffn(s, gains[2], mod[6:9], w1[1], w3[1], w2[1])


def setup_inputs(seed: int = 0) -> dict:
    key = jax.random.key(seed)
    ks = jax.random.split(key, 25)
    f32 = jnp.float32
    D, F, G, P = D_MODEL, D_FF, SSM_GROUPS, SSM_STATE

    def nrm(k, shape, scale):
        return scale * jax.random.normal(k, shape, f32)

    lam_im0 = jnp.pi * jnp.arange(P, dtype=f32)
    return {
        "x": nrm(ks[0], (BATCH, SEQ, D), 1.0),
        "c": nrm(ks[1], (BATCH, D), 1.0),
        "ctx": nrm(ks[2], (BATCH, CTX_LEN, D), 1.0),
        "c_ctx": nrm(ks[3], (D,), 1.0),
        "w_ada": nrm(ks[4], (DEPTH, D, N_MOD * D), D ** -0.5),
        "b_ada": nrm(ks[5], (DEPTH, N_MOD * D), 0.01),
        "norm_gain": 1.0 + nrm(ks[6], (DEPTH, 3, D), 0.02),
        "ffn_w1": nrm(ks[7], (DEPTH, 2, D, F), D ** -0.5),
        "ffn_w3": nrm(ks[8], (DEPTH, 2, D, F), D ** -0.5),
        "ffn_w2": nrm(ks[9], (DEPTH, 2, F, D), F ** -0.5),
        "w_in": nrm(ks[10], (N_EVEN, D, IN_WIDTH), D ** -0.5),
        "q_gain": 1.0 + nrm(ks[11], (N_EVEN, HEAD_DIM), 0.02),
        "k_gain": 1.0 + nrm(ks[12], (N_EVEN, HEAD_DIM), 0.02),
        "sink_logit": nrm(ks[13], (N_EVEN, N_Q_HEADS), 0.5),
        "ssm_lam_re": -0.5 + nrm(ks[14], (N_EVEN, 2, G, P), 0.01),
        "ssm_lam_im": lam_im0 + nrm(ks[15], (N_EVEN, 2, G, P), 0.01),
        "ssm_log_dt": jax.random.uniform(ks[16], (N_EVEN, 2, G), f32, math.log(DT_MIN), math.log(DT_MAX)),
        "ssm_b_re": nrm(ks[17], (N_EVEN, 2, G, P, SSM_GROUP), (2 * SSM_GROUP) ** -0.5),
        "ssm_b_im": nrm(ks[18], (N_EVEN, 2, G, P, SSM_GROUP), (2 * SSM_GROUP) ** -0.5),
        "ssm_c_re": nrm(ks[19], (N_EVEN, 2, G, SSM_GROUP, P), P ** -0.5),
        "ssm_c_im": nrm(ks[20], (N_EVEN, 2, G, SSM_GROUP, P), P ** -0.5),
        "ssm_d": nrm(ks[21], (N_EVEN, SSM_WIDTH), 1.0),
        "ssm_w_glu": nrm(ks[22], (N_EVEN, SSM_WIDTH, SSM_WIDTH), SSM_WIDTH ** -0.5),
        "w_out": nrm(ks[23], (N_EVEN, D, D), D ** -0.5),
        "fourier_w_out": nrm(ks[24], (N_ODD, D, D), D ** -0.5),
    }


def reference(x, c, ctx, c_ctx, w_ada, b_ada, norm_gain, ffn_w1, ffn_w3, ffn_w2, w_in, q_gain, k_gain,
              sink_logit, ssm_lam_re, ssm_lam_im, ssm_log_dt, ssm_b_re, ssm_b_im, ssm_c_re, ssm_c_im,
              ssm_d, ssm_w_glu, w_out, fourier_w_out):
    n_lat = x.shape[1]
    rows = n_lat // GRID_W
    cos, sin = axial_rope_tables(rows)
    s_ctx = ctx
    for layer in range(DEPTH):
        even = layer % 2 == 0
        ctx_later = any(j % 2 == 0 for j in range(layer + 1, DEPTH))
        gains = norm_gain[layer]
        w1, w3, w2 = ffn_w1[layer], ffn_w3[layer], ffn_w2[layer]
        m = ada_mod(c, w_ada[layer], b_ada[layer])
        if even:
            e = layer // 2
            mc = ada_mod(c_ctx, w_ada[layer], b_ada[layer])
            x = half_ffn(x, gains[0], m[0:3], w1[0], w3[0], w2[0])
            s_ctx = half_ffn(s_ctx, gains[0], mc[0:3], w1[0], w3[0], w2[0])
            h = modulate(rms_norm(x, gains[1]), m[3], m[4])
            hc = modulate(rms_norm(s_ctx, gains[1]), mc[3], mc[4])
            y, yc = attn_ssm_mixer(h, hc, cos, sin, w_in[e], q_gain[e], k_gain[e], sink_logit[e],
                                   ssm_lam_re[e], ssm_lam_im[e], ssm_log_dt[e], ssm_b_re[e], ssm_b_im[e],
                                   ssm_c_re[e], ssm_c_im[e], ssm_d[e], ssm_w_glu[e], w_out[e], ctx_later)
            x = x + m[5] * y
            x = half_ffn(x, gains[2], m[6:9], w1[1], w3[1], w2[1])
            if ctx_later:
                s_ctx = s_ctx + mc[5] * yc
                s_ctx = half_ffn(s_ctx, gains[2], mc[6:9], w1[1], w3[1], w2[1])
        else:
            o = layer // 2
            x = odd_layer_stream(x, gains, m, w1, w3, w2, fourier_w_out[o])
            if ctx_later:
                mc = ada_mod(c_ctx, w_ada[layer], b_ada[layer])
                s_ctx = odd_layer_stream(s_ctx, gains, mc, w1, w3, w2, fourier_w_out[o])
    return x
```

```python
import math
import os
import numpy as np
import concourse.bass as bass
import concourse.mybir as mybir
from concourse.bass_utils import run_bass_kernel_spmd

F32 = mybir.dt.float32
BF16 = mybir.dt.bfloat16
AF = mybir.ActivationFunctionType
ALU = mybir.AluOpType
AX = mybir.AxisListType

D = 1024
NT = 4096
NCTX = 256
FF = 2816
NFG = 11
EPS = 1e-6
HD = 64
NQH = 8
NKV = 2
GRID_W = 64
TWO_PI = 2.0 * math.pi
MAGIC = 12582912.0
CW1 = 6.28125
CW2 = TWO_PI - CW1

SAME_ENGINE_SYNC = os.environ.get("SES", "1") == "1"
ADA_BF16 = os.environ.get("ADA_BF16", "0") == "1"
STQ = os.environ.get("STQ", "gpsimd")


class Buf:
    def __init__(self, h, name):
        self.h = h
        self.name = name
        self.w = {}
        self.wacc = {}
        self.r = {}
        self.sem = None
        self.sem_sw = False
        self.cnt = 0
        self.excl = False
        self.inh = {}

    def __getitem__(self, idx):
        return self.h[idx]

    def ap(self):
        return self.h.ap()


class Eng:
    def __init__(self, name, sem, is_pe=False):
        self.name = name
        self.sem = sem
        self.count = 0
        self.seen = {}
        self.ops = []
        self.is_pe = is_pe


class Prog:
    def __init__(self, nc):
        self.nc = nc
        self.eng = {}
        for name in ("tensor", "vector", "scalar", "gpsimd", "sync"):
            self.eng[name] = Eng(name, nc.alloc_semaphore("s_" + name), is_pe=(name == "tensor"))
        self.nsem = 5
        self.inherit = {}
        self.scopes = []
        self.free_sems = {False: [], True: []}

    def _reg(self, b):
        b.inh = dict(self.inherit)
        if self.scopes:
            self.scopes[-1][1].append(b)
        return b

    def sbuf(self, name, shape, dtype):
        self.uid = getattr(self, "uid", 0) + 1
        name = "%s_%d" % (name, self.uid)
        if self.scopes:
            cm = self.nc.sbuf_tensor(name, list(shape), dtype)
            h = cm.__enter__()
            self.scopes[-1][0].append(cm)
        else:
            h = self.nc.alloc_sbuf_tensor(name, list(shape), dtype)
        return self._reg(Buf(h, name))

    def psum(self, name, shape, dtype=F32):
        b = Buf(self.nc.alloc_psum_tensor(name, list(shape), dtype), name)
        b.excl = True
        return b

    def dram(self, name, shape, dtype, kind="Internal"):
        return Buf(self.nc.dram_tensor(name, list(shape), dtype, kind=kind), name)

    def push(self):
        self.scopes.append(([], []))

    def pop(self):
        cms, bufs = self.scopes.pop()
        for b in bufs:
            if b.sem is not None:
                self.free_sems[b.sem_sw].append((b.sem, b.cnt))
            for d in (b.w, b.wacc, b.r, b.inh):
                for s, v in d.items():
                    if self.inherit.get(s, 0) < v:
                        self.inherit[s] = v
        for cm in reversed(cms):
            cm.__exit__(None, None, None)

    def _waits(self, E, reads, writes, awrites):
        waits = {}

        def merge(d):
            for s, v in d.items():
                if waits.get(s, 0) < v:
                    waits[s] = v
        for b in reads:
            merge(b.w)
            merge(b.wacc)
            merge(b.inh)
            if b.excl:
                merge(b.r)
        for b in writes:
            merge(b.w)
            merge(b.wacc)
            merge(b.r)
            merge(b.inh)
        for b in awrites:
            merge(b.w)
            merge(b.r)
            merge(b.inh)
        wl = []
        for s, v in waits.items():
            if s is E.sem and (E.is_pe or not SAME_ENGINE_SYNC):
                continue
            if E.seen.get(s, 0) >= v:
                continue
            E.seen[s] = v
            wl.append((s, v))
        return wl

    def _record(self, sem, val, reads, writes, awrites):
        for b in reads:
            if b.r.get(sem, 0) < val:
                b.r[sem] = val
        for b in writes:
            b.w = {sem: val}
            b.wacc = {}
            b.r = {}
        for b in awrites:
            if b.wacc.get(sem, 0) < val:
                b.wacc[sem] = val

    def op(self, e, fn, reads=(), writes=(), awrites=()):
        E = self.eng[e]
        wl = self._waits(E, reads, writes, awrites)
        E.count += 1
        E.ops.append((wl, fn, E.sem, 1))
        self._record(E.sem, E.count, reads, writes, awrites)

    def dma(self, q, out_ap, in_ap, reads=(), writes=(), awrites=(), sembuf=None, **kw):
        E = self.eng[q]
        wl = self._waits(E, reads, writes, awrites)
        if sembuf.sem is None:
            sembuf.sem_sw = (q == "gpsimd")
            if self.free_sems[sembuf.sem_sw]:
                sembuf.sem, sembuf.cnt = self.free_sems[sembuf.sem_sw].pop()
            else:
                sembuf.sem = self.nc.alloc_semaphore("d_" + sembuf.name)
                self.nsem += 1
        sembuf.cnt += 16
        E.ops.append((wl, lambda eng: eng.dma_start(out=out_ap, in_=in_ap, **kw), sembuf.sem, 16))
        self._record(sembuf.sem, sembuf.cnt, reads, writes, awrites)

    def final_wait(self, q, bufs):
        E = self.eng[q]
        wl = self._waits(E, (), bufs, ())
        E.ops.append((wl, None, None, 0))

    def emit(self):
        nc = self.nc

        def run(E):
            def f(eng):
                for wl, fn, sem, inc in E.ops:
                    for s, v in wl:
                        eng.wait_ge(s, v)
                    if fn is not None:
                        fn(eng).then_inc(sem, inc)
            return f
        with nc.Block() as block:
            block.sync(run(self.eng["sync"]))
            block.scalar(run(self.eng["scalar"]))
            block.vector(run(self.eng["vector"]))
            block.gpsimd(run(self.eng["gpsimd"]))
            block.tensor(run(self.eng["tensor"]))


class Rot:
    def __init__(self, bufs):
        self.bufs = bufs
        self.i = 0

    def get(self):
        b = self.bufs[self.i % len(self.bufs)]
        self.i += 1
        return b


def sub_last(a, start, step, count):
    dims = [list(d) for d in a.ap]
    st, n = dims[-1]
    dims[-1] = [st * step, count]
    return bass.AP(a.tensor, a.offset + st * start, dims)


def rev_last(a):
    dims = [list(d) for d in a.ap]
    st, n = dims[-1]
    dims[-1] = [-st, n]
    return bass.AP(a.tensor, a.offset + st * (n - 1), dims)


class K:
    def __init__(self, taps=(), stop_after=None, start_at=None):
        self.taps = set(taps)
        self.stop_after = stop_after
        self.start_at = start_at
        self.skip_ssm = False
        self.nc = bass.Bass("TRN2", target_bir_lowering=False)
        self.P = Prog(self.nc)
        self.inputs = {}
        self.outputs = {}

    def din(self, name, shape, dtype=F32):
        b = self.P.dram(name, shape, dtype, kind="ExternalInput")
        self.inputs[name] = b
        return b

    def dout(self, name, shape, dtype=F32):
        b = self.P.dram(name, shape, dtype, kind="ExternalOutput")
        self.outputs[name] = b
        return b

    def declare(self):
        P = self.P
        self.x = self.din("x", [NT, D])
        self.ctx = self.din("ctx", [NCTX, D])
        self.cvec = self.din("cvec", [2, D])
        self.w_ada = self.din("w_ada", [2, D, 9 * D])
        self.b_ada = self.din("b_ada", [2, 9 * D])
        self.norm_gain = self.din("norm_gain", [2, 3, D])
        self.ffn_w1 = self.din("ffn_w1", [2, 2, D, FF])
        self.ffn_w3 = self.din("ffn_w3", [2, 2, D, FF])
        self.ffn_w2 = self.din("ffn_w2", [2, 2, FF, D])
        self.w_in = self.din("w_in", [D, 1280])
        self.q_gain = self.din("q_gain", [1, HD])
        self.k_gain = self.din("k_gain", [1, HD])
        self.sink = self.din("sink_logit", [1, NQH])
        self.lam_re = self.din("ssm_lam_re", [2, 32, 64])
        self.lam_im = self.din("ssm_lam_im", [2, 32, 64])
        self.log_dt = self.din("ssm_log_dt", [2, 32])
        self.b_re = self.din("ssm_b_re", [2, 32, 64, 16])
        self.b_im = self.din("ssm_b_im", [2, 32, 64, 16])
        self.c_re = self.din("ssm_c_re", [2, 32, 16, 64])
        self.c_im = self.din("ssm_c_im", [2, 32, 16, 64])
        self.ssm_d = self.din("ssm_d", [1, 512])
        self.w_glu = self.din("ssm_w_glu", [512, 512])
        self.w_out = self.din("w_out", [D, D])
        self.w_f = self.din("fourier_w_out", [D, D])
        self.c_ident = self.din("c_ident", [128, 128])
        self.c_rope = self.din("c_rope", [NT, 64])
        self.c_mask = self.din("c_mask", [2, 128, 128])
        self.c_iota = self.din("c_iota", [1, 512])
        self.c_dftc = self.din("c_dftc", [2, 256, 256])
        self.out = self.dout("out", [NT, D])
        self.mrow = P.dram("mrow", [2, 2, 9 * D], F32)
        self.xs = [P.dram("xs%d" % i, [NT, D], F32) for i in range(5)]
        self.ctxs = P.dram("ctxs", [NCTX, D], F32)
        self.w1s = [[P.dram("w1s%d%d" % (l, h), [NFG, 128, 8, 256], BF16) for h in range(2)] for l in range(2)]
        self.w3s = [[P.dram("w3s%d%d" % (l, h), [NFG, 128, 8, 256], BF16) for h in range(2)] for l in range(2)]
        self.w2s = [[P.dram("w2s%d%d" % (l, h), [NFG, 128, 2, D], BF16) for h in range(2)] for l in range(2)]

    def tap(self, name, buf_ap, shape, reads, dtype=F32):
        if name not in self.taps:
            return
        o = self.dout("tap_" + name, shape, dtype)
        self.P.dma("sync", o.ap(), buf_ap, reads=reads, writes=[o], sembuf=reads[0])

    def setup(self):
        P = self.P
        self.ident_f = P.sbuf("ident_f", [128, 128], F32)
        self.ident_b = P.sbuf("ident_b", [128, 128], BF16)
        self.epsc = P.sbuf("epsc", [128, 1], F32)
        P.dma("sync", self.ident_f[:, :], self.c_ident.ap(), reads=[self.c_ident], writes=[self.ident_f], sembuf=self.ident_f)
        P.op("vector", lambda e: e.tensor_copy(out=self.ident_b[:, :], in_=self.ident_f[:, :]), reads=[self.ident_f], writes=[self.ident_b])
        P.op("vector", lambda e: e.memset(self.epsc[:, :], EPS), writes=[self.epsc])
        self.ps = [P.psum("ps%d" % i, [128, 512], F32) for i in range(8)]

    def convert_ffn(self, l, h):
        P = self.P
        w1 = self.ffn_w1.ap()[l, h].rearrange("(kc p) (fg f) -> fg p kc f", p=128, f=256)
        w3 = self.ffn_w3.ap()[l, h].rearrange("(kc p) (fg f) -> fg p kc f", p=128, f=256)
        w2 = self.ffn_w2.ap()[l, h].rearrange("(fg fc p) d -> fg p fc d", p=128, fc=2)
        for fg in range(NFG):
            P.dma("gpsimd", self.w1s[l][h][fg], w1[fg], reads=[self.ffn_w1], awrites=[self.w1s[l][h]], sembuf=self.w1s[l][h])
            P.dma("gpsimd", self.w3s[l][h][fg], w3[fg], reads=[self.ffn_w3], awrites=[self.w3s[l][h]], sembuf=self.w3s[l][h])
        for fg in range(NFG):
            P.dma("gpsimd", self.w2s[l][h][fg], w2[fg], reads=[self.ffn_w2], awrites=[self.w2s[l][h]], sembuf=self.w2s[l][h])

    def ada_phase(self):
        P = self.P
        P.push()
        ctmp = P.sbuf("ctmp", [128, 2, 8], F32)
        ADA_DT = BF16 if ADA_BF16 else F32
        cc2 = P.sbuf("cc2", [128, 8, 128], ADA_DT)
        P.op("vector", lambda e: e.memset(cc2[:, :, :], 0.0), writes=[cc2])
        for v in range(2):
            P.dma("sync", ctmp[:, v, :], self.cvec.ap()[v].rearrange("(p k) -> p k", k=8), reads=[self.cvec],
                  awrites=[ctmp], sembuf=ctmp)
        for v in range(2):
            P.op("scalar", lambda e, v=v: e.activation(out=cc2[:, :, v], in_=ctmp[:, v, :], func=AF.Silu),
                 reads=[ctmp], awrites=[cc2])
        was = Rot([P.sbuf("wa%d" % i, [128, 8, 512], ADA_DT) for i in range(4 if ADA_BF16 else 3)])
        bada = P.sbuf("bada", [2, 9 * D], F32)
        mr = P.sbuf("mr", [2, 9 * D], F32)
        pss = Rot(self.ps[0:2])
        for l in range(2):
            P.dma("sync", bada[:, :], self.b_ada.ap()[l:l + 1, :].partition_broadcast(2), reads=[self.b_ada], writes=[bada], sembuf=bada)
            wsrc = self.w_ada.ap()[l].rearrange("(p k) n -> p k n", k=8)
            for nt in range(18):
                wa = was.get()
                P.dma("gpsimd" if ADA_BF16 else "sync", wa[:, :, :], wsrc[:, :, nt * 512:(nt + 1) * 512], reads=[self.w_ada], writes=[wa], sembuf=wa)
                ps = pss.get()
                for kc in range(8):
                    P.op("tensor", lambda e, ps=ps, wa=wa, kc=kc: e.matmul(ps[:, :], lhsT=cc2[:, kc, :],
                                                                       rhs=wa[:, kc, :],
                                                                       start=(kc == 0), stop=(kc == 7)),
                         reads=[cc2, wa], writes=[ps] if kc == 0 else [], awrites=[] if kc == 0 else [ps])
                P.op("vector", lambda e, ps=ps, nt=nt: e.tensor_tensor(out=mr[:, nt * 512:(nt + 1) * 512], in0=ps[0:2, :],
                                                                      in1=bada[:, nt * 512:(nt + 1) * 512], op=ALU.add),
                     reads=[ps, bada], awrites=[mr])
            P.dma("scalar", self.mrow.ap()[l], mr[:, :], reads=[mr], writes=[] , awrites=[self.mrow], sembuf=mr)
            if "mrow" in self.taps:
                o_ = self.dout("tap_mrow%d" % l, [2, 9 * D], F32)
                P.dma("sync", o_.ap(), mr[:, :], reads=[mr], writes=[o_], sembuf=mr)
        P.pop()

    def load_rows(self, pref, l, v, site, gmul, need_g=True):
        P = self.P
        S = P.sbuf(pref + "S", [128, D], F32)
        Sh = P.sbuf(pref + "Sh", [128, D], F32)
        m = self.mrow.ap()[l, v]
        base = 3 * site * D
        P.dma("sync", Sh[:, :], m[base:base + D].partition_broadcast(128), reads=[self.mrow], writes=[Sh], sembuf=Sh)
        P.dma("sync", S[:, :], m[base + D:base + 2 * D].partition_broadcast(128), reads=[self.mrow], writes=[S], sembuf=S)
        G = None
        if need_g:
            G = P.sbuf(pref + "G", [128, D], F32)
            P.dma("sync", G[:, :], m[base + 2 * D:base + 3 * D].partition_broadcast(128), reads=[self.mrow], writes=[G], sembuf=G)
        gn = P.sbuf(pref + "gn", [128, D], F32)
        P.dma("sync", gn[:, :], self.norm_gain.ap()[l, site].partition_broadcast(128), reads=[self.norm_gain], writes=[gn], sembuf=gn)
        P.op("vector", lambda e: e.scalar_tensor_tensor(out=S[:, :], in0=S[:, :], scalar=1.0, in1=gn[:, :], op0=ALU.add, op1=ALU.mult),
             reads=[S, gn], writes=[S])
        if need_g and gmul != 1.0:
            P.op("vector", lambda e: e.tensor_scalar(out=G[:, :], in0=G[:, :], scalar1=gmul, scalar2=None, op0=ALU.mult),
                 reads=[G], writes=[G])
        return S, Sh, G

    def make_s1_bufs(self, pref, small=False):
        P = self.P
        b = {}
        b["xt"] = Rot([P.sbuf(pref + "xt%d" % i, [128, D], F32) for i in range(2 if small else 5)])
        b["junk"] = P.sbuf(pref + "junk", [128, D], BF16)
        b["t1"] = Rot([P.sbuf(pref + "t1%d" % i, [128, D], F32) for i in range(1 if small else 2)])
        b["xm"] = Rot([P.sbuf(pref + "xm%d" % i, [128, D], BF16) for i in range(2 if small else 4)])
        b["st"] = Rot([P.sbuf(pref + "st%d" % i, [128, 4], F32) for i in range(4)])
        return b

    def stage1(self, b, src_buf, src_ap, S, Sh, xnT, col0, pstr):
        xm = self.stage1a(b, src_buf, src_ap, S, Sh)
        self.stage1b(xm, xnT, col0, pstr)

    def stage1a(self, b, src_buf, src_ap, S, Sh):
        P = self.P
        xt = b["xt"].get()
        P.dma("sync", xt[:, :], src_ap, reads=[src_buf], writes=[xt], sembuf=xt)
        st = b["st"].get()
        junk = b["junk"]
        P.op("scalar", lambda e: e.activation(out=junk[:, :], in_=xt[:, :], func=AF.Square, accum_out=st[:, 0:1]),
             reads=[xt], writes=[junk, st])
        P.op("scalar", lambda e: e.activation(out=st[:, 1:2], in_=st[:, 0:1], func=AF.Sqrt, bias=self.epsc[:, 0:1], scale=1.0 / D),
             reads=[st, self.epsc], writes=[st])
        P.op("vector", lambda e: e.reciprocal(out=st[:, 2:3], in_=st[:, 1:2]), reads=[st], writes=[st])
        t1 = b["t1"].get()
        P.op("vector", lambda e: e.scalar_tensor_tensor(out=t1[:, :], in0=xt[:, :], scalar=st[:, 2:3], in1=S[:, :], op0=ALU.mult, op1=ALU.mult),
             reads=[xt, st, S], writes=[t1])
        xm = b["xm"].get()
        P.op("gpsimd", lambda e: e.tensor_tensor(out=xm[:, :], in0=t1[:, :], in1=Sh[:, :], op=ALU.add), reads=[t1, Sh], writes=[xm])
        return xm

    def stage1b(self, xm, xnT, col0, pstr):
        P = self.P
        for half in range(2):
            ps = pstr.get()
            for c in range(4):
                kc = half * 4 + c
                P.op("tensor", lambda e, ps=ps, c=c, kc=kc: e.matmul(ps[:, c * 128:(c + 1) * 128], lhsT=xm[:, kc * 128:(kc + 1) * 128],
                                                                    rhs=self.ident_b[:, :], start=True, stop=True),
                     reads=[xm, self.ident_b], writes=[ps] if c == 0 else [], awrites=[] if c == 0 else [ps])
            dst = xnT[:, half * 4:half * 4 + 4, col0:col0 + 128]
            src = ps[:, :].rearrange("p (a b) -> p a b", a=4)
            if half == 0:
                P.op("scalar", lambda e, dst=dst, src=src: e.activation(out=dst, in_=src, func=AF.Copy), reads=[ps], awrites=[xnT])
            else:
                P.op("vector", lambda e, dst=dst, src=src: e.tensor_copy(out=dst, in_=src), reads=[ps], awrites=[xnT])

    def ffn_phase(self, name, l, h, site, jobs):
        P = self.P
        P.push()
        w2r = P.sbuf(name + "w2r", [128, 22, D], BF16)
        for fg in range(NFG):
            P.dma("sync", w2r[:, 2 * fg:2 * fg + 2, :], self.w2s[l][h][fg], reads=[self.w2s[l][h]], awrites=[w2r], sembuf=w2r)
        rows = {}
        for v in sorted(set(j[3] for j in jobs)):
            rows[v] = self.load_rows(name + "r%d" % v, l, v, site, 0.5)
        s1b = self.make_s1_bufs(name)
        xnTs = Rot([P.sbuf(name + "xnT%d" % i, [128, 8, 512], BF16) for i in range(2)])
        g = P.sbuf(name + "g", [128, 22, 512], BF16)
        w1t = Rot([P.sbuf(name + "w1t%d" % i, [128, 8, 256], BF16) for i in range(3)])
        w3t = Rot([P.sbuf(name + "w3t%d" % i, [128, 8, 256], BF16) for i in range(3)])
        sl = Rot([P.sbuf(name + "sl%d" % i, [128, 512], F32) for i in range(2)])
        yt = Rot([P.sbuf(name + "yt%d" % i, [128, 512], F32) for i in range(2)])
        xr = Rot([P.sbuf(name + "xr%d" % i, [128, D], F32) for i in range(2)])
        ot = Rot([P.sbuf(name + "ot%d" % i, [128, D], F32) for i in range(2)])
        pstr = Rot(self.ps[0:2])
        psh = Rot(self.ps[2:6])
        psy = Rot(self.ps[6:8])
        tiles = []
        for (src, dst, ntok, v) in jobs:
            for t0 in range(0, ntok, 512):
                tiles.append((src, dst, t0, min(512, ntok - t0), v))

        def S1a(tile):
            src, dst, t0, n, v = tile
            S, Sh, G = rows[v]
            return [self.stage1a(s1b, src, src.ap()[t0 + j * 128:t0 + (j + 1) * 128, :], S, Sh) for j in range(n // 128)]

        def S1b(xms):
            xnT = xnTs.get()
            for j, xm in enumerate(xms):
                self.stage1b(xm, xnT, j * 128, pstr)
            return xnT

        def S2(tile, xnT):
            n = tile[3]
            for fg in range(NFG):
                a = w1t.get()
                b3 = w3t.get()
                P.dma("sync", a[:, :, :], self.w1s[l][h][fg], reads=[self.w1s[l][h]], writes=[a], sembuf=a)
                P.dma("sync", b3[:, :, :], self.w3s[l][h][fg], reads=[self.w3s[l][h]], writes=[b3], sembuf=b3)
                for fc in range(2):
                    p1 = psh.get()
                    p3 = psh.get()
                    for (pp, wt) in ((p1, a), (p3, b3)):
                        for kc in range(8):
                            P.op("tensor", lambda e, pp=pp, wt=wt, kc=kc, fc=fc: e.matmul(pp[:, 0:n], lhsT=wt[:, kc, fc * 128:(fc + 1) * 128],
                                                                                       rhs=xnT[:, kc, 0:n], start=(kc == 0), stop=(kc == 7)),
                                 reads=[wt, xnT], writes=[pp] if kc == 0 else [], awrites=[] if kc == 0 else [pp])
                    s = sl.get()
                    P.op("scalar", lambda e, s=s, p1=p1: e.activation(out=s[:, 0:n], in_=p1[:, 0:n], func=AF.Silu), reads=[p1], writes=[s])
                    fi = fg * 2 + fc
                    P.op("vector", lambda e, s=s, p3=p3, fi=fi: e.tensor_tensor(out=g[:, fi, 0:n], in0=s[:, 0:n], in1=p3[:, 0:n], op=ALU.mult),
                         reads=[s, p3], awrites=[g])

        def S3(tile):
            src, dst, t0, n, v = tile
            S, Sh, G = rows[v]
            for j in range(n // 128):
                x_r = xr.get()
                P.dma("sync", x_r[:, :], src.ap()[t0 + j * 128:t0 + (j + 1) * 128, :], reads=[src], writes=[x_r], sembuf=x_r)
                o = ot.get()
                for dh in range(2):
                    py = psy.get()
                    for fi in range(22):
                        P.op("tensor", lambda e, py=py, fi=fi, j=j, dh=dh: e.matmul(py[:, :], lhsT=g[:, fi, j * 128:(j + 1) * 128],
                                                                                 rhs=w2r[:, fi, dh * 512:(dh + 1) * 512], start=(fi == 0), stop=(fi == 21)),
                             reads=[g, w2r], writes=[py] if fi == 0 else [], awrites=[] if fi == 0 else [py])
                    y = yt.get()
                    P.op("vector", lambda e, y=y, py=py, dh=dh: e.tensor_tensor(out=y[:, :], in0=py[:, :], in1=G[:, dh * 512:(dh + 1) * 512], op=ALU.mult),
                         reads=[py, G], writes=[y])
                    P.op("gpsimd", lambda e, y=y, o=o, x_r=x_r, dh=dh: e.tensor_tensor(out=o[:, dh * 512:(dh + 1) * 512], in0=y[:, :],
                                                                                   in1=x_r[:, dh * 512:(dh + 1) * 512], op=ALU.add),
                         reads=[y, x_r], writes=[o] if dh == 0 else [], awrites=[] if dh == 0 else [o])
                P.dma(STQ, dst.ap()[t0 + j * 128:t0 + (j + 1) * 128, :], o[:, :], reads=[o], awrites=[dst], sembuf=o)

        cur = S1b(S1a(tiles[0]))
        for i, tile in enumerate(tiles):
            S2(tile, cur)
            xms = S1a(tiles[i + 1]) if i + 1 < len(tiles) else None
            S3(tile)
            cur = S1b(xms) if xms is not None else None
        P.pop()


    def mixer_phase(self, name, src, csrc, dst):
        P = self.P
        l, site = 0, 1
        P.push()
        attnT = P.sbuf(name + "attnT", [128, 4, NT], BF16)
        uT = [P.sbuf(name + "uT%d" % i, [128, 4, 256 if i == 0 else 512], BF16) for i in range(9)]
        self.mix_attn(name, src, csrc, attnT, uT)
        if "attnT" in self.taps:
            o = self.dout("tap_attnT", [128, 4, NT], BF16)
            P.dma("sync", o.ap(), attnT[:, :, :], reads=[attnT], writes=[o], sembuf=attnT)
        if "uT" in self.taps:
            o = self.dout("tap_uT", [128, 4, 256 + NT], BF16)
            for i in range(9):
                c0 = 0 if i == 0 else 256 + (i - 1) * 512
                P.dma("sync", o.ap()[:, :, c0:c0 + (256 if i == 0 else 512)], uT[i][:, :, :], reads=[uT[i]], awrites=[o], sembuf=uT[i])
        if not self.skip_ssm:
            self.mix_ssm(name, uT)
        if "ssmT" in self.taps:
            o = self.dout("tap_ssmT", [128, 4, NT], BF16)
            for i in range(1, 9):
                P.dma("sync", o.ap()[:, :, (i - 1) * 512:i * 512], uT[i][:, :, :], reads=[uT[i]], awrites=[o], sembuf=uT[i])
        if "noout" not in os.environ.get("MIXDBG", ""):
            self.mix_out(name, src, dst, attnT, uT)
        P.pop()

    def mix_attn(self, name, src, csrc, attnT, uT):
        P = self.P
        l, site = 0, 1
        P.push()
        kT = P.sbuf(name + "kT", [128, NCTX + NT], BF16)
        V = P.sbuf(name + "V", [128, 34, 2, 65], BF16)
        P.op("vector", lambda e: e.memset(V[:, :, :, :], 1.0), writes=[V])
        P.push()
        win = P.sbuf(name + "win", [128, 8, 1280], BF16)
        P.dma("gpsimd", win[:, :, :], self.w_in.ap().rearrange("(k p) n -> p k n", p=128), reads=[self.w_in], writes=[win], sembuf=win)
        rows = {}
        for v in range(2):
            rows[v] = self.load_rows(name + "r%d" % v, l, v, site, 1.0, need_g=False)
        s1b = self.make_s1_bufs(name, small=True)
        hTs = Rot([P.sbuf(name + "hT%d" % i, [128, 8, 128], BF16) for i in range(2)])
        gq = P.sbuf(name + "gq", [128, 10, 64], F32)
        P.dma("sync", gq[:, 0, :], self.q_gain.ap()[0:1, :].partition_broadcast(128), reads=[self.q_gain], awrites=[gq], sembuf=gq)
        P.dma("sync", gq[:, 8, :], self.k_gain.ap()[0:1, :].partition_broadcast(128), reads=[self.k_gain], awrites=[gq], sembuf=gq)
        P.op("vector", lambda e: e.tensor_scalar(out=gq[:, 0, :], in0=gq[:, 0, :], scalar1=0.125, scalar2=None, op0=ALU.mult), reads=[gq], awrites=[gq])
        for hh in range(1, 8):
            P.op("vector", lambda e, hh=hh: e.tensor_copy(out=gq[:, hh, :], in_=gq[:, 0, :]), reads=[gq], awrites=[gq])
        P.op("vector", lambda e: e.tensor_copy(out=gq[:, 9, :], in_=gq[:, 8, :]), reads=[gq], awrites=[gq])
        esink = P.sbuf(name + "esink", [128, 8], F32)
        P.dma("sync", esink[:, :], self.sink.ap()[0:1, :].partition_broadcast(128), reads=[self.sink], writes=[esink], sembuf=esink)
        P.op("scalar", lambda e: e.activation(out=esink[:, :], in_=esink[:, :], func=AF.Exp), reads=[esink], writes=[esink])
        mk = P.sbuf(name + "mk", [128, 2, 128], BF16)
        P.dma("gpsimd", mk[:, :, :], self.c_mask.ap().rearrange("m i r -> i m r"), reads=[self.c_mask], writes=[mk], sembuf=mk)
        qk = Rot([P.sbuf(name + "qk%d" % i, [128, 10, 64], F32) for i in range(2)])
        sq = P.sbuf(name + "sq", [128, 10, 64], F32)
        qn = Rot([P.sbuf(name + "qn%d" % i, [128, 10, 64], F32) for i in range(2)])
        ta = P.sbuf(name + "ta", [128, 10, 32], F32)
        tb = P.sbuf(name + "tb", [128, 10, 32], F32)
        tc = P.sbuf(name + "tc", [128, 10, 32], F32)
        td = P.sbuf(name + "td", [128, 10, 32], F32)
        qkr = Rot([P.sbuf(name + "qkr%d" % i, [128, 10, 64], BF16) for i in range(2)])
        rp = Rot([P.sbuf(name + "rp%d" % i, [128, 64], F32) for i in range(2)])
        sm = Rot([P.sbuf(name + "sm%d" % i, [128, 32], F32) for i in range(4)])
        ub = Rot([P.sbuf(name + "ub%d" % i, [128, 512], BF16) for i in range(2)])
        qTs = Rot([P.sbuf(name + "qT%d" % i, [128, 4, 128], BF16) for i in range(3)])
        pTs = Rot([P.sbuf(name + "pT%d" % i, [128, 5, 512], BF16) for i in range(int(os.environ.get("NPT", "2")))])
        Ot = Rot([P.sbuf(name + "Ot%d" % i, [128, 8, 64], BF16) for i in range(2)])
        psr = Rot(self.ps)

        def mm(ps_ap, lhsT, rhs, first, last, reads, psb, first_in_bank):
            P.op("tensor", lambda e: e.matmul(ps_ap, lhsT=lhsT, rhs=rhs, start=first, stop=last),
                 reads=reads, writes=[psb] if first_in_bank else [], awrites=[] if first_in_bank else [psb])

        def pre(blk, is_ctx):
            v = 1 if is_ctx else 0
            S, Sh, _ = rows[v]
            sbuf_src = csrc if is_ctx else src
            r0 = blk * 128 if is_ctx else (blk - 2) * 128
            return self.stage1a(s1b, sbuf_src, sbuf_src.ap()[r0:r0 + 128, :], S, Sh)

        def project(blk, is_ctx, xm, res):
            hT = hTs.get()
            r0 = blk * 128 if is_ctx else (blk - 2) * 128
            self.stage1b(xm, hT, 0, psr)
            yield
            h0 = 8 if is_ctx else 0
            nh = 10 - h0
            q_k = qk.get()
            if not is_ctx:
                pq = psr.get()
                for kc in range(8):
                    mm(pq[:, :], hT[:, kc, :], win[:, kc, 0:512], kc == 0, kc == 7, [hT, win], pq, kc == 0)
                P.op("scalar", lambda e, pq=pq, q_k=q_k: e.activation(
                    out=q_k[:, 0:8, :].rearrange("p (a hi) d -> p hi a d", hi=2),
                    in_=pq[:, :].rearrange("p (hi a d) -> p hi a d", hi=2, a=4), func=AF.Copy), reads=[pq], writes=[q_k])
            yield
            pkv = psr.get()
            for kc in range(8):
                mm(pkv[:, 0:256], hT[:, kc, :], win[:, kc, 512:768], kc == 0, kc == 7, [hT, win], pkv, kc == 0)
            P.op("scalar", lambda e, pkv=pkv, q_k=q_k: e.activation(out=q_k[:, 8:10, :], in_=pkv[:, 0:128].rearrange("p (h d) -> p h d", h=2), func=AF.Copy),
                 reads=[pkv], writes=[q_k] if is_ctx else [], awrites=[] if is_ctx else [q_k])
            P.op("vector", lambda e, pkv=pkv, blk=blk: e.tensor_copy(out=V[:, blk, :, 0:64], in_=pkv[:, 128:256].rearrange("p (h d) -> p h d", h=2)),
                 reads=[pkv], awrites=[V])
            yield
            pu = psr.get()
            for kc in range(8):
                mm(pu[:, :], hT[:, kc, :], win[:, kc, 768:1280], kc == 0, kc == 7, [hT, win], pu, kc == 0)
            u_b = ub.get()
            P.op("scalar", lambda e, pu=pu, u_b=u_b: e.activation(out=u_b[:, :], in_=pu[:, :], func=AF.Copy), reads=[pu], writes=[u_b])
            put = psr.get()
            for c in range(4):
                mm(put[:, c * 128:(c + 1) * 128], u_b[:, c * 128:(c + 1) * 128], self.ident_b[:, :], True, True, [u_b, self.ident_b], put, c == 0)
            if is_ctx:
                ut, c0 = uT[0], blk * 128
            else:
                ut, c0 = uT[1 + (blk - 2) // 4], ((blk - 2) % 4) * 128
            P.op("vector", lambda e, put=put, ut=ut, c0=c0: e.tensor_copy(out=ut[:, :, c0:c0 + 128], in_=put[:, :].rearrange("p (a b) -> p a b", a=4)),
                 reads=[put], awrites=[ut])
            yield
            hs = slice(h0, 10)
            sm_ = sm.get()
            P.op("scalar", lambda e: e.activation(out=sq[:, hs, :], in_=q_k[:, hs, :], func=AF.Square), reads=[q_k], writes=[sq])
            P.op("vector", lambda e: e.tensor_reduce(out=sm_[:, h0:10], in_=sq[:, hs, :], axis=AX.X, op=ALU.add), reads=[sq], writes=[sm_])
            P.op("scalar", lambda e: e.activation(out=sm_[:, 10 + h0:20], in_=sm_[:, h0:10], func=AF.Sqrt, bias=self.epsc[:, 0:1], scale=1.0 / HD),
                 reads=[sm_, self.epsc], writes=[sm_])
            P.op("vector", lambda e: e.reciprocal(out=sm_[:, 20 + h0:30], in_=sm_[:, 10 + h0:20]), reads=[sm_], writes=[sm_])
            yield
            q_n = qn.get()
            P.op("vector", lambda e: e.tensor_tensor(out=q_n[:, hs, :], in0=q_k[:, hs, :], in1=sm_[:, 20 + h0:30].unsqueeze(2).to_broadcast([128, nh, 64]), op=ALU.mult),
                 reads=[q_k, sm_], writes=[q_n])
            q_r = qkr.get()
            if is_ctx:
                P.op("gpsimd", lambda e: e.tensor_tensor(out=q_r[:, hs, :], in0=q_n[:, hs, :], in1=gq[:, hs, :], op=ALU.mult), reads=[q_n, gq], writes=[q_r])
            else:
                P.op("gpsimd", lambda e: e.tensor_tensor(out=q_n[:, :, :], in0=q_n[:, :, :], in1=gq[:, :, :], op=ALU.mult), reads=[q_n, gq], writes=[q_n])
                yield
                r_p = rp.get()
                P.dma("sync", r_p[:, :], self.c_rope.ap()[r0:r0 + 128, :], reads=[self.c_rope], writes=[r_p], sembuf=r_p)
                cosb = r_p[:, 0:32].unsqueeze(1).to_broadcast([128, 10, 32])
                sinb = r_p[:, 32:64].unsqueeze(1).to_broadcast([128, 10, 32])
                x1 = q_n[:, :, 0:32]
                x2 = q_n[:, :, 32:64]
                P.op("vector", lambda e: e.tensor_tensor(out=ta[:, :, :], in0=x1, in1=cosb, op=ALU.mult), reads=[q_n, r_p], writes=[ta])
                P.op("gpsimd", lambda e: e.tensor_tensor(out=tb[:, :, :], in0=x2, in1=sinb, op=ALU.mult), reads=[q_n, r_p], writes=[tb])
                P.op("vector", lambda e: e.tensor_tensor(out=q_r[:, :, 0:32], in0=ta[:, :, :], in1=tb[:, :, :], op=ALU.subtract), reads=[ta, tb], writes=[q_r])
                P.op("gpsimd", lambda e: e.tensor_tensor(out=tc[:, :, :], in0=x1, in1=sinb, op=ALU.mult), reads=[q_n, r_p], writes=[tc])
                P.op("vector", lambda e: e.tensor_tensor(out=td[:, :, :], in0=x2, in1=cosb, op=ALU.mult), reads=[q_n, r_p], writes=[td])
                P.op("gpsimd", lambda e: e.tensor_tensor(out=q_r[:, :, 32:64], in0=tc[:, :, :], in1=td[:, :, :], op=ALU.add), reads=[tc, td], awrites=[q_r])
            yield
            pk = psr.get()
            mm(pk[:, 0:128], q_r[:, 8:10, :], self.ident_b[:, :], True, True, [q_r, self.ident_b], pk, True)
            kcol = blk * 128 if is_ctx else NCTX + (blk - 2) * 128
            P.op("vector", lambda e: e.tensor_copy(out=kT[:, kcol:kcol + 128], in_=pk[:, 0:128]), reads=[pk], awrites=[kT])
            if is_ctx:
                return
            yield
            pqt = psr.get()
            for a in range(4):
                mm(pqt[:, a * 128:(a + 1) * 128], q_r[:, 2 * a:2 * a + 2, :], self.ident_b[:, :], True, True, [q_r, self.ident_b], pqt, a == 0)
            qT = qTs.get()
            P.op("scalar", lambda e: e.activation(out=qT[:, :, :], in_=pqt[:, :].rearrange("p (a b) -> p a b", a=4), func=AF.Copy), reads=[pqt], writes=[qT])
            res["qT"] = qT

        def attend(i, qT):
            kbs = [("c", 0, 0), ("c", 1, 1)]
            if i > 0:
                kbs.append(("p", 2 + i - 1, 2 + i - 1))
            kbs.append(("m", 2 + i, 2 + i))
            if i < 31:
                kbs.append(("n", 2 + i + 1, 2 + i + 1))
            O = Ot.get()
            if i == 1 and "qT" in self.taps:
                o_ = self.dout("tap_qT", [128, 4, 128], BF16)
                P.dma("sync", o_.ap(), qT[:, :, :], reads=[qT], writes=[o_], sembuf=o_)
                o_ = self.dout("tap_kT", [128, NCTX + NT], BF16)
                P.dma("sync", o_.ap(), kT[:, :], reads=[kT], writes=[o_], sembuf=o_)
                o_ = self.dout("tap_V", [128, 34, 2, 65], BF16)
                P.dma("sync", o_.ap(), V[:, :, :, :], reads=[V], writes=[o_], sembuf=o_)
            for hk in range(2):
                pT = pTs.get()
                pr = slice(hk * 64, (hk + 1) * 64)
                for bi, (kind, kcb, vb) in enumerate(kbs):
                    pss = psr.get()
                    mm(pss[:, :], kT[pr, kcb * 128:(kcb + 1) * 128], qT[pr, :, :], True, True, [kT, qT], pss, True)
                    if i == 1 and "qT" in self.taps and bi == 0:
                        sc_ = P.sbuf(name + "sc", [128, 512], F32)
                        P.op("vector", lambda e, pss=pss, sc_=sc_: e.tensor_copy(out=sc_[:, :], in_=pss[:, :]), reads=[pss], writes=[sc_])
                        o_ = self.dout("tap_sc%d" % hk, [128, 512], F32)
                        P.dma("sync", o_.ap(), sc_[:, :], reads=[sc_], writes=[o_], sembuf=o_)
                    P.op("scalar", lambda e, pss=pss, bi=bi, pT=pT: e.activation(out=pT[:, bi, :], in_=pss[:, :], func=AF.Exp), reads=[pss],
                         writes=[pT] if bi == 0 else [], awrites=[] if bi == 0 else [pT])
                    if bi % 2 == 1:
                        yield
                    if kind in ("p", "n"):
                        mi = 0 if kind == "p" else 1
                        P.op("gpsimd", lambda e, bi=bi, mi=mi, pT=pT: e.tensor_tensor(
                            out=pT[:, bi, :].rearrange("p (a r) -> p a r", a=4), in0=pT[:, bi, :].rearrange("p (a r) -> p a r", a=4),
                            in1=mk[:, mi, :].unsqueeze(1).to_broadcast([128, 4, 128]), op=ALU.mult), reads=[pT, mk], awrites=[pT])
                if i == 1 and "qT" in self.taps:
                    o_ = self.dout("tap_pT%d" % hk, [128, 5, 512], BF16)
                    P.dma("sync", o_.ap(), pT[:, :, :], reads=[pT], writes=[o_], sembuf=o_)
                po = psr.get()
                for a in range(4):
                    for bi, (kind, kcb, vb) in enumerate(kbs):
                        mm(po[:, a * 128:a * 128 + 65], pT[:, bi, a * 128:(a + 1) * 128], V[:, vb, hk, :], bi == 0, bi == len(kbs) - 1,
                           [pT, V], po, a == 0 and bi == 0)
                yield
                sm_ = sm.get()
                pov = po[:, :].rearrange("p (a c) -> p a c", a=4)
                P.op("vector", lambda e, pov=pov, sm_=sm_, hk=hk: e.tensor_tensor(out=sm_[:, 0:4], in0=pov[:, :, 64], in1=esink[:, hk * 4:hk * 4 + 4], op=ALU.add),
                     reads=[po, esink], writes=[sm_])
                P.op("vector", lambda e, sm_=sm_: e.reciprocal(out=sm_[:, 4:8], in_=sm_[:, 0:4]), reads=[sm_], writes=[sm_])
                P.op("vector", lambda e, pov=pov, sm_=sm_, hk=hk: e.tensor_tensor(out=O[:, hk * 4:hk * 4 + 4, :], in0=pov[:, :, 0:64],
                                                                               in1=sm_[:, 4:8].unsqueeze(2).to_broadcast([128, 4, 64]), op=ALU.mult),
                     reads=[po, sm_], writes=[O] if hk == 0 else [], awrites=[] if hk == 0 else [O])
            yield
            pot = psr.get()
            for c in range(4):
                mm(pot[:, c * 128:(c + 1) * 128], O[:, 2 * c:2 * c + 2, :], self.ident_b[:, :], True, True, [O, self.ident_b], pot, c == 0)
            P.op("scalar", lambda e: e.activation(out=attnT[:, :, i * 128:(i + 1) * 128], in_=pot[:, :].rearrange("p (a b) -> p a b", a=4), func=AF.Copy),
                 reads=[pot], awrites=[attnT])

        def run(*gens):
            gens = [g for g in gens if g is not None]
            while gens:
                for g in list(gens):
                    try:
                        next(g)
                    except StopIteration:
                        gens.remove(g)

        xm0 = pre(0, True)
        xm1 = pre(1, True)
        run(project(0, True, xm0, {}))
        xmn = pre(2, False)
        run(project(1, True, xm1, {}))
        qTs_by_blk = {}
        for j in range(32):
            xm = xmn
            xmn = pre(2 + j + 1, False) if j < 31 else None
            res = {}
            run(project(2 + j, False, xm, res), attend(j - 2, qTs_by_blk.pop(j - 2)) if j >= 2 else None)
            qTs_by_blk[j] = res["qT"]
        run(attend(30, qTs_by_blk.pop(30)))
        run(attend(31, qTs_by_blk.pop(31)))
        P.pop()
        P.pop()


    def tt(self, eng, out, in0, in1, op, reads, writes=(), awrites=()):
        self.P.op(eng, lambda e: e.tensor_tensor(out=out, in0=in0, in1=in1, op=op), reads, writes, awrites)

    def ts(self, eng, out, in0, s1, s2, op0, op1=None, reads=(), writes=(), awrites=()):
        if op1 is None:
            self.P.op(eng, lambda e: e.tensor_scalar(out=out, in0=in0, scalar1=s1, scalar2=None, op0=op0), reads, writes, awrites)
        else:
            self.P.op(eng, lambda e: e.tensor_scalar(out=out, in0=in0, scalar1=s1, scalar2=s2, op0=op0, op1=op1), reads, writes, awrites)

    def stt(self, eng, out, in0, scalar, in1, op0, op1, reads, writes=(), awrites=()):
        self.P.op(eng, lambda e: e.scalar_tensor_tensor(out=out, in0=in0, scalar=scalar, in1=in1, op0=op0, op1=op1), reads, writes, awrites)

    def act(self, out, in_, func, reads, writes=(), awrites=(), **kw):
        self.P.op("scalar", lambda e: e.activation(out=out, in_=in_, func=func, **kw), reads, writes, awrites)

    def cp(self, eng, out, in_, reads, writes=(), awrites=()):
        self.P.op(eng, lambda e: e.tensor_copy(out=out, in_=in_), reads, writes, awrites)

    def mmul(self, ps_ap, lhsT, rhs, first, last, reads, psb, first_in_bank):
        self.P.op("tensor", lambda e: e.matmul(ps_ap, lhsT=lhsT, rhs=rhs, start=first, stop=last),
                  reads=reads, writes=[psb] if first_in_bank else [], awrites=[] if first_in_bank else [psb])

    def sin_reduced(self, out_b, out_ap, phi_b, phi_ap, add, kk, r, shape_sl, eng="vector"):
        k_ap, r_ap = shape_sl(kk), shape_sl(r)
        if add != 0.0:
            self.ts(eng, r_ap, phi_ap, add, None, ALU.add, reads=[phi_b], writes=[r])
            src_b, src_ap = r, r_ap
        else:
            src_b, src_ap = phi_b, phi_ap
        self.ts(eng, k_ap, src_ap, 1.0 / TWO_PI, MAGIC, ALU.mult, ALU.add, reads=[src_b], writes=[kk])
        self.ts(eng, k_ap, k_ap, -MAGIC, None, ALU.add, reads=[kk], writes=[kk])
        if eng == "vector":
            self.stt(eng, r_ap, k_ap, -CW1, src_ap, ALU.mult, ALU.add, reads=[kk, src_b], writes=[r])
            self.stt(eng, r_ap, k_ap, -CW2, r_ap, ALU.mult, ALU.add, reads=[kk, r], writes=[r])
        else:
            assert add != 0.0
            self.ts(eng, shape_sl(self._sr_tmp), k_ap, -CW1, None, ALU.mult, reads=[kk], writes=[self._sr_tmp])
            self.tt(eng, r_ap, r_ap, shape_sl(self._sr_tmp), ALU.add, [r, self._sr_tmp], [r])
            self.ts(eng, k_ap, k_ap, -CW2, None, ALU.mult, reads=[kk], writes=[kk])
            self.tt(eng, r_ap, r_ap, k_ap, ALU.add, [r, kk], [r])
        self.ts(eng, r_ap, r_ap, -math.pi, math.pi, ALU.max, ALU.min, reads=[r], writes=[r])
        self.act(out_ap, r_ap, AF.Sin, reads=[r], writes=[out_b])

    def mix_ssm(self, name, uT):
        P = self.P
        P.push()
        psr = Rot(self.ps)
        R = P.sbuf(name + "R", [64, 128], F32)
        for a, arr in enumerate((self.lam_re, self.lam_im)):
            for d in range(2):
                P.dma("sync", R[(a * 2 + d) * 16:(a * 2 + d + 1) * 16, :], arr.ap()[d].rearrange("(k gl) p -> k (gl p)", gl=2),
                      reads=[arr], awrites=[R], sembuf=R)
        pl = psr.get()
        self.mmul(pl[:, 0:64], R[:, :], self.ident_f[0:64, 0:64], True, True, [R, self.ident_f], pl, True)
        LL = P.sbuf(name + "LL", [128, 64], F32)
        self.cp("vector", LL[:, :], pl[:, 0:64], [pl], [LL])
        T2 = P.sbuf(name + "T2", [32, 2], F32)
        P.dma("sync", T2[:, :], self.log_dt.ap().rearrange("d (k gl) -> (d k) gl", gl=2), reads=[self.log_dt], writes=[T2], sembuf=T2)
        E2 = P.sbuf(name + "E2", [32, 128], F32)
        self.cp("vector", E2[:, :].rearrange("r (g p) -> r g p", g=2), T2[:, :].unsqueeze(2).to_broadcast([32, 2, 64]), [T2], [E2])
        pd = psr.get()
        self.mmul(pd[:, 0:32], E2[:, :], self.ident_f[0:32, 0:32], True, True, [E2, self.ident_f], pd, True)
        sc = {}
        for nm in ("dt", "t", "rho", "th", "sn", "cs", "ar", "ai", "nr", "den", "fr", "fi", "u1", "u2", "kk", "rr", "th2", "rho2"):
            sc[nm] = P.sbuf(name + "p_" + nm, [128, 32], F32)
        full = lambda b: b[:, :]
        lr, li = LL[:, 0:32], LL[:, 32:64]
        self.act(sc["dt"][:, :], pd[:, 0:32], AF.Exp, [pd], [sc["dt"]])
        self.tt("vector", sc["t"][:, :], lr, sc["dt"][:, :], ALU.mult, [LL, sc["dt"]], [sc["t"]])
        self.act(sc["rho"][:, :], sc["t"][:, :], AF.Exp, [sc["t"]], [sc["rho"]])
        self.tt("vector", sc["th"][:, :], li, sc["dt"][:, :], ALU.mult, [LL, sc["dt"]], [sc["th"]])
        self.sin_reduced(sc["sn"], sc["sn"][:, :], sc["th"], sc["th"][:, :], 0.0, sc["kk"], sc["rr"], full)
        self.sin_reduced(sc["cs"], sc["cs"][:, :], sc["th"], sc["th"][:, :], math.pi / 2, sc["kk"], sc["rr"], full)
        self.tt("vector", sc["ar"][:, :], sc["rho"][:, :], sc["cs"][:, :], ALU.mult, [sc["rho"], sc["cs"]], [sc["ar"]])
        self.tt("vector", sc["ai"][:, :], sc["rho"][:, :], sc["sn"][:, :], ALU.mult, [sc["rho"], sc["sn"]], [sc["ai"]])
        self.ts("vector", sc["nr"][:, :], sc["ar"][:, :], -1.0, None, ALU.add, reads=[sc["ar"]], writes=[sc["nr"]])
        self.tt("vector", sc["u1"][:, :], lr, lr, ALU.mult, [LL], [sc["u1"]])
        self.tt("vector", sc["u2"][:, :], li, li, ALU.mult, [LL], [sc["u2"]])
        self.tt("vector", sc["den"][:, :], sc["u1"][:, :], sc["u2"][:, :], ALU.add, [sc["u1"], sc["u2"]], [sc["den"]])
        P.op("vector", lambda e: e.reciprocal(out=sc["den"][:, :], in_=sc["den"][:, :]), reads=[sc["den"]], writes=[sc["den"]])
        self.tt("vector", sc["u1"][:, :], sc["nr"][:, :], lr, ALU.mult, [sc["nr"], LL], [sc["u1"]])
        self.tt("vector", sc["u2"][:, :], sc["ai"][:, :], li, ALU.mult, [sc["ai"], LL], [sc["u2"]])
        self.tt("vector", sc["fr"][:, :], sc["u1"][:, :], sc["u2"][:, :], ALU.add, [sc["u1"], sc["u2"]], [sc["fr"]])
        self.tt("vector", sc["fr"][:, :], sc["fr"][:, :], sc["den"][:, :], ALU.mult, [sc["fr"], sc["den"]], [sc["fr"]])
        self.tt("vector", sc["u1"][:, :], sc["ai"][:, :], lr, ALU.mult, [sc["ai"], LL], [sc["u1"]])
        self.tt("vector", sc["u2"][:, :], sc["nr"][:, :], li, ALU.mult, [sc["nr"], LL], [sc["u2"]])
        self.tt("vector", sc["fi"][:, :], sc["u1"][:, :], sc["u2"][:, :], ALU.subtract, [sc["u1"], sc["u2"]], [sc["fi"]])
        self.tt("vector", sc["fi"][:, :], sc["fi"][:, :], sc["den"][:, :], ALU.mult, [sc["fi"], sc["den"]], [sc["fi"]])
        self.ts("vector", sc["th2"][:, :], sc["th"][:, :], 2.0, None, ALU.mult, reads=[sc["th"]], writes=[sc["th2"]])
        self.tt("vector", sc["rho2"][:, :], sc["rho"][:, :], sc["rho"][:, :], ALU.mult, [sc["rho"]], [sc["rho2"]])
        Bmat = P.sbuf(name + "Bmat", [128, 2, 16, 2, 2, 128], BF16)
        Cmat = P.sbuf(name + "Cmat", [128, 2, 16, 2, 2, 128], BF16)
        K0s = P.sbuf(name + "K0s", [128, 2, 4, 128], BF16)
        P.push()
        CR2 = Rot([P.sbuf(name + "CR2%d" % i, [128, 4, 128], F32) for i in range(2)])
        P.op("vector", lambda e: e.memset(Cmat[:, :, :, :, :, :], 0.0), writes=[Cmat])
        for d in range(2):
            for ri, arr in enumerate((self.c_re, self.c_im)):
                C2 = CR2.get()
                for dup in range(2):
                    P.dma("sync", C2[:, :, dup * 64:(dup + 1) * 64], arr.ap()[d].rearrange("(c g8) j p -> (g8 j) c p", c=4), reads=[arr],
                          writes=[C2] if dup == 0 else [], awrites=[] if dup == 0 else [C2], sembuf=C2)
                pc = psr.get()
                for cq in range(4):
                    self.mmul(pc[:, cq * 128:(cq + 1) * 128], C2[:, cq, :], self.ident_f[:, :], True, True, [C2, self.ident_f], pc, cq == 0)
                for gl in range(2):
                    for kl in range(4):
                        rows = slice(gl * 64, (gl + 1) * 64)
                        c0 = 32 * kl + 16 * gl
                        dst = Cmat[rows, d, :, 0, ri, :].rearrange("q (c k) x -> q c k x", k=4)[:, :, kl, c0:c0 + 16]
                        srcv = pc[rows, :].rearrange("q (c x) -> q c x", c=4)[:, :, c0:c0 + 16]
                        if ri == 0:
                            self.cp("vector", dst, srcv, [pc], awrites=[Cmat])
                        else:
                            self.ts("vector", dst, srcv, -1.0, None, ALU.mult, reads=[pc], awrites=[Cmat])
        ctmp_ = P.sbuf(name + "ctmpv", [128, 128], F32)
        for d in range(2):
            for k in range(16):
                dk = d * 16 + k
                X0, X1 = Cmat[:, d, k, 0, 0, :], Cmat[:, d, k, 0, 1, :]
                ar_, ai_ = sc["ar"][:, dk:dk + 1], sc["ai"][:, dk:dk + 1]
                self.ts("vector", ctmp_[:, :], X1, ai_, None, ALU.mult, reads=[Cmat, sc["ai"]], writes=[ctmp_])
                self.stt("vector", Cmat[:, d, k, 1, 0, :], X0, ar_, ctmp_[:, :], ALU.mult, ALU.add, reads=[Cmat, sc["ar"], ctmp_], awrites=[Cmat])
                self.ts("vector", ctmp_[:, :], X0, ai_, None, ALU.mult, reads=[Cmat, sc["ai"]], writes=[ctmp_])
                self.stt("vector", Cmat[:, d, k, 1, 1, :], X1, ar_, ctmp_[:, :], ALU.mult, ALU.subtract, reads=[Cmat, sc["ar"], ctmp_], awrites=[Cmat])
        Bre = P.sbuf(name + "Bre", [128, 32, 16], F32)
        Bim = P.sbuf(name + "Bim", [128, 32, 16], F32)
        for d in range(2):
            P.dma("sync", Bre[:, d * 16:(d + 1) * 16, :], self.b_re.ap()[d].rearrange("(k gl) p i -> (gl p) k i", gl=2), reads=[self.b_re], awrites=[Bre], sembuf=Bre)
            P.dma("sync", Bim[:, d * 16:(d + 1) * 16, :], self.b_im.ap()[d].rearrange("(k gl) p i -> (gl p) k i", gl=2), reads=[self.b_im], awrites=[Bim], sembuf=Bim)
        bbr = P.sbuf(name + "bbr", [128, 32, 16], F32)
        bbi = P.sbuf(name + "bbi", [128, 32, 16], F32)
        tmp = P.sbuf(name + "btmp", [128, 32, 16], F32)
        frb = sc["fr"][:, :].unsqueeze(2).to_broadcast([128, 32, 16])
        fib = sc["fi"][:, :].unsqueeze(2).to_broadcast([128, 32, 16])
        self.tt("vector", bbr[:, :, :], Bre[:, :, :], frb, ALU.mult, [Bre, sc["fr"]], [bbr])
        self.tt("vector", tmp[:, :, :], Bim[:, :, :], fib, ALU.mult, [Bim, sc["fi"]], [tmp])
        self.tt("vector", bbr[:, :, :], bbr[:, :, :], tmp[:, :, :], ALU.subtract, [bbr, tmp], [bbr])
        self.tt("vector", bbi[:, :, :], Bim[:, :, :], frb, ALU.mult, [Bim, sc["fr"]], [bbi])
        self.tt("vector", tmp[:, :, :], Bre[:, :, :], fib, ALU.mult, [Bre, sc["fi"]], [tmp])
        self.tt("vector", bbi[:, :, :], bbi[:, :, :], tmp[:, :, :], ALU.add, [bbi, tmp], [bbi])
        bbr1 = P.sbuf(name + "bbr1", [128, 32, 16], F32)
        bbi1 = P.sbuf(name + "bbi1", [128, 32, 16], F32)
        arb = sc["ar"][:, :].unsqueeze(2).to_broadcast([128, 32, 16])
        aib = sc["ai"][:, :].unsqueeze(2).to_broadcast([128, 32, 16])
        self.tt("vector", bbr1[:, :, :], bbr[:, :, :], arb, ALU.mult, [bbr, sc["ar"]], [bbr1])
        self.tt("vector", tmp[:, :, :], bbi[:, :, :], aib, ALU.mult, [bbi, sc["ai"]], [tmp])
        self.tt("vector", bbr1[:, :, :], bbr1[:, :, :], tmp[:, :, :], ALU.subtract, [bbr1, tmp], [bbr1])
        self.tt("vector", bbi1[:, :, :], bbi[:, :, :], arb, ALU.mult, [bbi, sc["ar"]], [bbi1])
        self.tt("vector", tmp[:, :, :], bbr[:, :, :], aib, ALU.mult, [bbr, sc["ai"]], [tmp])
        self.tt("vector", bbi1[:, :, :], bbi1[:, :, :], tmp[:, :, :], ALU.add, [bbi1, tmp], [bbi1])
        Eall = Rot([P.sbuf(name + "Eall%d" % i, [128, 16, 128], BF16) for i in range(2)])
        Ekeep = {}
        for d, m_, ri, bb in [(d, m_, ri, ((bbr, bbi), (bbr1, bbi1))[m_][ri]) for d in range(2) for m_ in range(2) for ri in range(2)]:
            if True:
                E = Eall.get()
                P.op("vector", lambda e, E=E: e.memset(E[:, :, :], 0.0), writes=[E])
                for gl in range(2):
                    for kl in range(4):
                        rows = slice(gl * 64, (gl + 1) * 64)
                        c0 = 32 * kl + 16 * gl
                        dst = E[rows, :, :].rearrange("q (c k) x -> q c k x", k=4)[:, :, kl, c0:c0 + 16]
                        srcv = bb[rows, d * 16:(d + 1) * 16, :].rearrange("q (c k) i -> q c k i", k=4)[:, :, kl, :]
                        self.cp("vector", dst, srcv, [bb], awrites=[E])
                for k4 in range(4):
                    pb = psr.get()
                    for j in range(4):
                        k = k4 * 4 + j
                        self.mmul(pb[:, j * 128:(j + 1) * 128], E[:, k, :], self.ident_b[:, :], True, True, [E, self.ident_b], pb, j == 0)
                    self.act(Bmat[:, d, k4 * 4:(k4 + 1) * 4, m_, ri, :], pb[:, :].rearrange("p (a b) -> p a b", a=4), AF.Copy, [pb], awrites=[Bmat])
                if m_ == 0:
                    Ekeep[ri] = E
                    if ri == 1:
                        pk0 = psr.get()
                        for cq in range(4):
                            i_ = 0
                            for kl in range(4):
                                for r2 in range(2):
                                    self.mmul(pk0[:, cq * 128:(cq + 1) * 128], Ekeep[r2][:, cq * 4 + kl, :], Cmat[:, d, cq * 4 + kl, 0, r2, :],
                                              i_ == 0, i_ == 7, [Ekeep[r2], Cmat], pk0, cq == 0 and i_ == 0)
                                    i_ += 1
                        self.act(K0s[:, d, :, :], pk0[:, :].rearrange("p (a b) -> p a b", a=4), AF.Copy, [pk0], awrites=[K0s])
        P.pop()
        dcol = P.sbuf(name + "dcol", [128, 4], F32)
        for cq in range(4):
            P.dma("sync", dcol[:, cq:cq + 1], self.ssm_d.ap()[0, cq * 128:(cq + 1) * 128].rearrange("(p o) -> p o", o=1), reads=[self.ssm_d], awrites=[dcol], sembuf=dcol)
        Dg = P.sbuf(name + "Dg", [128, 4, 128], BF16)
        for cq in range(4):
            self.ts("vector", Dg[:, cq, :], self.ident_f[:, :], dcol[:, cq:cq + 1], None, ALU.mult, reads=[self.ident_f, dcol], awrites=[Dg])
        wg = P.sbuf(name + "wg", [128, 4, 512], BF16)
        P.dma("gpsimd", wg[:, :, :], self.w_glu.ap().rearrange("(k p) n -> p k n", p=128), reads=[self.w_glu], writes=[wg], sembuf=wg)
        IO = P.sbuf(name + "IO", [128, 512], F32)
        P.dma("sync", IO[:, :], self.c_iota.ap()[0:1, :].partition_broadcast(128), reads=[self.c_iota], writes=[IO], sembuf=IO)
        yf = P.dram(name + "yf", [4, 128, NT], F32)
        N2 = 256
        cs_t = [P.sbuf(name + "cs%d" % i, [128, N2], F32) for i in range(4)]
        sn_t = [P.sbuf(name + "sn%d" % i, [128, N2], F32) for i in range(4)]
        rho_t = [P.sbuf(name + "rho%d" % i, [128, N2], F32) for i in range(4)]
        hst = [P.sbuf(name + "hst%d" % i, [128, 2], F32) for i in range(4)]
        phi = P.sbuf(name + "phi", [128, N2], F32)
        kk = P.sbuf(name + "kk", [128, N2], F32)
        rr = P.sbuf(name + "rr", [128, N2], F32)
        kk2 = P.sbuf(name + "kk2", [128, N2], F32)
        self._sr_tmp = P.sbuf(name + "srtmp", [128, N2], F32)
        rr2 = P.sbuf(name + "rr2", [128, N2], F32)
        WA = [P.sbuf(name + "wA%d" % i, [128, N2], F32) for i in range(4)]
        WB = [P.sbuf(name + "wB%d" % i, [128, N2], F32) for i in range(4)]
        WC = [P.sbuf(name + "wC%d" % i, [128, N2], F32) for i in range(4)]
        WD = [P.sbuf(name + "wD%d" % i, [128, N2], F32) for i in range(4)]
        HR = [P.sbuf(name + "hbr%d" % i, [128, N2 + 1], BF16) for i in range(4)]
        HI = [P.sbuf(name + "hbi%d" % i, [128, N2 + 1], BF16) for i in range(4)]
        yfs = Rot([P.sbuf(name + "yfs%d" % i, [128, 512], F32) for i in range(2)])
        yts = Rot([P.sbuf(name + "yts%d" % i, [128, 512], F32) for i in range(2)])
        gl1 = Rot([P.sbuf(name + "gl1%d" % i, [128, 512], F32) for i in range(2)])
        gl2 = Rot([P.sbuf(name + "gl2%d" % i, [128, 512], F32) for i in range(2)])
        slN2 = lambda b_: b_[:, :]
        psx = Rot(self.ps[0:4])
        psyy = Rot(self.ps[4:8])

        def scan_op(out_b, rho, in_b, h_, col, n):
            P.op("vector", lambda e: e.tensor_tensor_scan(out=out_b[:, 0:n], data0=rho[:, 0:n], data1=in_b[:, 0:n],
                                                          initial=h_[:, col:col + 1], op0=ALU.mult, op1=ALU.add),
                 reads=[rho, in_b, h_], writes=[out_b])

        for d in range(2):
            for c in range(4):
                for kl in range(4):
                    dk = d * 16 + c * 4 + kl
                    self.ts("vector", phi[:, :], IO[:, 0:N2], sc["th2"][:, dk:dk + 1], None, ALU.mult, reads=[IO, sc["th2"]], writes=[phi])
                    self.sin_reduced(sn_t[kl], sn_t[kl][:, :], phi, phi[:, :], 0.0, kk, rr, slN2)
                    self.sin_reduced(cs_t[kl], cs_t[kl][:, :], phi, phi[:, :], math.pi / 2, kk, rr, slN2)
                    self.ts("gpsimd", rho_t[kl][:, :], IO[:, 0:N2], 0.0, sc["rho2"][:, dk:dk + 1], ALU.mult, ALU.add, reads=[IO, sc["rho2"]], writes=[rho_t[kl]])
                    P.op("vector", lambda e, h=hst[kl]: e.memset(h[:, :], 0.0), writes=[hst[kl]])
                order = [0] + (list(range(1, 9)) if d == 0 else list(range(8, 0, -1)))
                for n_i in order:
                    ut = uT[n_i]
                    n = 256 if n_i == 0 else 512
                    n2 = n // 2
                    ub_ = ut[:, c, :]
                    if d == 0:
                        u_ev, u_od = sub_last(ub_, 0, 2, n2), sub_last(ub_, 1, 2, n2)
                    else:
                        u_ev, u_od = sub_last(ub_, n - 1, -2, n2), sub_last(ub_, n - 2, -2, n2)
                    readout = n_i > 0
                    if readout:
                        pev, pod = psyy.get(), psyy.get()
                    for KL in ((0, 1), (2, 3)):
                        px = {}
                        for kl in KL:
                            k = c * 4 + kl
                            pxr, pxi = psx.get(), psx.get()
                            px[kl] = (pxr, pxi)
                            for ri, pacc in ((0, pxr), (1, pxi)):
                                self.mmul(pacc[:, 0:n2], Bmat[:, d, k, 1, ri, :], u_ev, True, False, [Bmat, ut], pacc, True)
                                self.mmul(pacc[:, 0:n2], Bmat[:, d, k, 0, ri, :], u_od, False, True, [Bmat, ut], pacc, False)
                        for kl in KL:
                            if readout:
                                self.act(HR[kl][:, 0:1], hst[kl][:, 0:1], AF.Copy, [hst[kl]], writes=[HR[kl]])
                                self.act(HI[kl][:, 0:1], hst[kl][:, 1:2], AF.Copy, [hst[kl]], writes=[HI[kl]])
                        for kl in KL:
                            pxr, pxi = px[kl]
                            cs, sn = cs_t[kl], sn_t[kl]
                            self.tt("vector", WA[kl][:, 0:n2], pxr[:, 0:n2], cs[:, 0:n2], ALU.mult, [pxr, cs], [WA[kl]])
                            self.tt("vector", WB[kl][:, 0:n2], pxi[:, 0:n2], sn[:, 0:n2], ALU.mult, [pxi, sn], [WB[kl]])
                            self.tt("vector", WC[kl][:, 0:n2], pxi[:, 0:n2], cs[:, 0:n2], ALU.mult, [pxi, cs], [WC[kl]])
                            self.tt("vector", WD[kl][:, 0:n2], pxr[:, 0:n2], sn[:, 0:n2], ALU.mult, [pxr, sn], [WD[kl]])
                        for kl in KL:
                            self.tt("gpsimd", WA[kl][:, 0:n2], WA[kl][:, 0:n2], WB[kl][:, 0:n2], ALU.add, [WA[kl], WB[kl]], [WA[kl]])
                            self.tt("gpsimd", WC[kl][:, 0:n2], WC[kl][:, 0:n2], WD[kl][:, 0:n2], ALU.subtract, [WC[kl], WD[kl]], [WC[kl]])
                        for kl in KL:
                            scan_op(WB[kl], rho_t[kl], WA[kl], hst[kl], 0, n2)
                            scan_op(WD[kl], rho_t[kl], WC[kl], hst[kl], 1, n2)
                        for kl in KL:
                            cs, sn = cs_t[kl], sn_t[kl]
                            self.tt("vector", WA[kl][:, 0:n2], WB[kl][:, 0:n2], cs[:, 0:n2], ALU.mult, [WB[kl], cs], [WA[kl]])
                            self.tt("gpsimd", WC[kl][:, 0:n2], WD[kl][:, 0:n2], sn[:, 0:n2], ALU.mult, [WD[kl], sn], [WC[kl]])
                            self.tt("vector", WB[kl][:, 0:n2], WB[kl][:, 0:n2], sn[:, 0:n2], ALU.mult, [WB[kl], sn], [WB[kl]])
                            self.tt("gpsimd", WD[kl][:, 0:n2], WD[kl][:, 0:n2], cs[:, 0:n2], ALU.mult, [WD[kl], cs], [WD[kl]])
                        for kl in KL:
                            self.tt("vector", WA[kl][:, 0:n2], WA[kl][:, 0:n2], WC[kl][:, 0:n2], ALU.subtract, [WA[kl], WC[kl]], [WA[kl]])
                            self.tt("gpsimd" if os.environ.get("HCOMBO_POOL") else "vector", WB[kl][:, 0:n2], WB[kl][:, 0:n2], WD[kl][:, 0:n2], ALU.add, [WB[kl], WD[kl]], [WB[kl]])
                        for kl in KL:
                            h_ = hst[kl]
                            self.act(h_[:, 0:1], WA[kl][:, n2 - 1:n2], AF.Copy, [WA[kl]], writes=[h_])
                            self.act(h_[:, 1:2], WB[kl][:, n2 - 1:n2], AF.Copy, [WB[kl]], awrites=[h_])
                            if readout:
                                self.act(HR[kl][:, 1:n2 + 1], WA[kl][:, 0:n2], AF.Copy, [WA[kl]], awrites=[HR[kl]])
                                self.act(HI[kl][:, 1:n2 + 1], WB[kl][:, 0:n2], AF.Copy, [WB[kl]], awrites=[HI[kl]])
                    if not readout:
                        continue
                    n_ev = 9 + (1 if d == 1 else 0)
                    n_od = 8 + (1 if d == 1 else 0)
                    i_e = i_o = 0
                    for kl in range(4):
                        k = c * 4 + kl
                        for ri, Hs in ((0, HR[kl]), (1, HI[kl])):
                            self.mmul(pod[:, 0:n2], Cmat[:, d, k, 0, ri, :], Hs[:, 1:n2 + 1], i_o == 0, i_o == n_od - 1, [Cmat, Hs], pod, i_o == 0)
                            i_o += 1
                            self.mmul(pev[:, 0:n2], Cmat[:, d, k, 1, ri, :], Hs[:, 0:n2], i_e == 0, i_e == n_ev - 1, [Cmat, Hs], pev, i_e == 0)
                            i_e += 1
                    self.mmul(pev[:, 0:n2], K0s[:, d, c, :], u_ev, False, i_e == n_ev - 1, [K0s, ut], pev, False)
                    i_e += 1
                    t0 = (n_i - 1) * 512
                    if d == 0:
                        yt_ = yts.get()
                        self.act(sub_last(yt_[:, :], 0, 2, n2), pev[:, 0:n2], AF.Copy, [pev], writes=[yt_])
                        self.act(sub_last(yt_[:, :], 1, 2, n2), pod[:, 0:n2], AF.Copy, [pod], awrites=[yt_])
                        P.dma("scalar", yf.ap()[c][:, t0:t0 + 512], yt_[:, :], reads=[yt_], awrites=[yf], sembuf=yt_)
                    else:
                        self.mmul(pev[:, 0:n2], Dg[:, c, :], u_ev, False, True, [Dg, ut], pev, False)
                        self.mmul(pod[:, 0:n2], Dg[:, c, :], u_od, False, True, [Dg, ut], pod, False)
                        yf_ = yfs.get()
                        P.dma("sync", yf_[:, :], yf.ap()[c][:, t0:t0 + 512], reads=[yf], writes=[yf_], sembuf=yf_)
                        yt_ = yts.get()
                        self.tt("vector", sub_last(yt_[:, :], n - 1, -2, n2), pev[:, 0:n2], sub_last(yf_[:, :], n - 1, -2, n2), ALU.add, [pev, yf_], writes=[yt_])
                        self.tt("vector", sub_last(yt_[:, :], n - 2, -2, n2), pod[:, 0:n2], sub_last(yf_[:, :], n - 2, -2, n2), ALU.add, [pod, yf_], awrites=[yt_])
                        g1, g2 = gl1.get(), gl2.get()
                        self.tt("gpsimd", g1[:, :], yt_[:, :], yt_[:, :], ALU.mult, [yt_], [g1])
                        self.ts("gpsimd", g1[:, :], g1[:, :], 0.044715, 1.0, ALU.mult, ALU.add, reads=[g1], writes=[g1])
                        self.tt("gpsimd", g1[:, :], g1[:, :], yt_[:, :], ALU.mult, [g1, yt_], [g1])
                        self.act(g2[:, :], g1[:, :], AF.Sigmoid, [g1], [g2], scale=2.0 * math.sqrt(2.0 / math.pi))
                        self.tt("vector", ut[:, c, :], yt_[:, :], g2[:, :], ALU.mult, [yt_, g2], awrites=[ut])
        if "yf" in self.taps:
            o_ = self.dout("tap_yf", [4, 128, NT], F32)
            P.dma("sync", o_.ap(), yf.ap(), reads=[yf], writes=[o_], sembuf=o_)
        sg = Rot([P.sbuf(name + "sg%d" % i, [128, 4, 512], BF16) for i in range(2)])
        for n_i in range(1, 9):
            ut = uT[n_i]
            s_ = sg.get()
            for m in range(4):
                pz = psr.get()
                for kc in range(4):
                    self.mmul(pz[:, :], wg[:, kc, m * 128:(m + 1) * 128], ut[:, kc, :], kc == 0, kc == 3, [wg, ut], pz, kc == 0)
                self.act(s_[:, m, :], pz[:, :], AF.Sigmoid, [pz], writes=[s_] if m == 0 else [], awrites=[] if m == 0 else [s_])
            self.tt("vector", ut[:, :, :], ut[:, :, :], s_[:, :, :], ALU.mult, [ut, s_], [ut])
        P.pop()

    def mix_out(self, name, src, dst, attnT, uT):
        P = self.P
        l, site = 0, 1
        P.push()
        wo = P.sbuf(name + "wo", [128, 8, D], BF16)
        P.dma("gpsimd", wo[:, :, :], self.w_out.ap().rearrange("(k p) n -> p k n", p=128), reads=[self.w_out], writes=[wo], sembuf=wo)
        G = P.sbuf(name + "G", [128, D], F32)
        P.dma("sync", G[:, :], self.mrow.ap()[l, 0][(3 * site + 2) * D:(3 * site + 3) * D].partition_broadcast(128), reads=[self.mrow], writes=[G], sembuf=G)
        xr = Rot([P.sbuf(name + "oxr%d" % i, [128, D], F32) for i in range(2)])
        ot = Rot([P.sbuf(name + "oot%d" % i, [128, D], F32) for i in range(2)])
        yt = Rot([P.sbuf(name + "oyt%d" % i, [128, 512], F32) for i in range(2)])
        psr = Rot(self.ps)
        for j in range(32):
            x_r = xr.get()
            P.dma("sync", x_r[:, :], src.ap()[j * 128:(j + 1) * 128, :], reads=[src], writes=[x_r], sembuf=x_r)
            o = ot.get()
            ut = uT[1 + j // 4]
            uc0 = (j % 4) * 128
            for dh in range(2):
                po = psr.get()
                for kc in range(8):
                    if kc < 4:
                        lhsT, rd = attnT[:, kc, j * 128:(j + 1) * 128], attnT
                    else:
                        lhsT, rd = ut[:, kc - 4, uc0:uc0 + 128], ut
                    if kc >= 4 and self.skip_ssm:
                        continue
                    last = (kc == 3) if self.skip_ssm else (kc == 7)
                    P.op("tensor", lambda e, po=po, lhsT=lhsT, kc=kc, dh=dh, last=last: e.matmul(po[:, :], lhsT=lhsT, rhs=wo[:, kc, dh * 512:(dh + 1) * 512],
                                                                                           start=(kc == 0), stop=last),
                         reads=[rd, wo], writes=[po] if kc == 0 else [], awrites=[] if kc == 0 else [po])
                t = yt.get()
                P.op("vector", lambda e, t=t, po=po, dh=dh: e.tensor_tensor(out=t[:, :], in0=po[:, :], in1=G[:, dh * 512:(dh + 1) * 512], op=ALU.mult),
                     reads=[po, G], writes=[t])
                P.op("gpsimd", lambda e, t=t, o=o, x_r=x_r, dh=dh: e.tensor_tensor(out=o[:, dh * 512:(dh + 1) * 512], in0=t[:, :],
                                                                               in1=x_r[:, dh * 512:(dh + 1) * 512], op=ALU.add),
                     reads=[t, x_r], writes=[o] if dh == 0 else [], awrites=[] if dh == 0 else [o])
            P.dma(STQ, dst.ap()[j * 128:(j + 1) * 128, :], o[:, :], reads=[o], awrites=[dst], sembuf=o)
        P.pop()

    def fourier_phase(self, name, l, site, src, dst):
        P = self.P
        if "c_dftn" not in self.inputs:
            self.c_dftn = self.din("c_dftn", [2, 32, 128, 32, 128], BF16)
        for half in range(2):
            self.fourier_half(name, l, site, src, dst, half)

    def fourier_half(self, name, l, site, src, dst, half):
        P = self.P
        if True:
            P.push()
            S, Sh, G = self.load_rows(name + "r", l, 0, site, 1.0)
            s1b = self.make_s1_bufs(name, small=True)
            dc = P.sbuf(name + "dc", [128, 2, 2, 256], BF16)
            for cs in range(2):
                P.dma("gpsimd", dc[:, cs, :, :], self.c_dftc.ap()[cs].rearrange("(k p) j -> p k j", p=128), reads=[self.c_dftc],
                      awrites=[dc], sembuf=dc)
            wf = P.sbuf(name + "wf", [128, 4, D], BF16)
            P.dma("gpsimd", wf[:, :, :], self.w_f.ap()[half * 512:(half + 1) * 512, :].rearrange("(k p) n -> p k n", p=128),
                  reads=[self.w_f], writes=[wf], sembuf=wf)
            AB = P.sbuf(name + "AB", [128, 32, 2, 512], BF16)
            hT = Rot([P.sbuf(name + "hT%d" % i, [128, 8, 128], BF16) for i in range(2)])
            pstr = Rot(self.ps[0:2])
            psab = Rot(self.ps[2:4])
            xm_next = self.stage1a(s1b, src, src.ap()[0:128, :], S, Sh)
            for nb in range(32):
                h = hT.get()
                xm_cur = xm_next
                if nb < 31:
                    xm_next = self.stage1a(s1b, src, src.ap()[(nb + 1) * 128:(nb + 2) * 128, :], S, Sh)
                self.stage1b(xm_cur, h, 0, pstr)
                for cs in range(2):
                    pa = psab.get()
                    for gq in range(2):
                        for kc2 in range(2):
                            kc = (half * 2 + gq) * 2 + kc2
                            P.op("tensor", lambda e, pa=pa, h=h, kc=kc, cs=cs, kc2=kc2, gq=gq: e.matmul(
                                pa[:, gq * 256:(gq + 1) * 256], lhsT=h[:, kc, :], rhs=dc[:, cs, kc2, :], start=(kc2 == 0), stop=(kc2 == 1)),
                                reads=[h, dc], writes=[pa] if (gq == 0 and kc2 == 0) else [], awrites=[] if (gq == 0 and kc2 == 0) else [pa])
                    if cs == 0:
                        P.op("scalar", lambda e, pa=pa, nb=nb: e.activation(out=AB[:, nb, 0, :], in_=pa[:, :], func=AF.Copy), reads=[pa], awrites=[AB])
                    else:
                        P.op("vector", lambda e, pa=pa, nb=nb: e.tensor_copy(out=AB[:, nb, 1, :], in_=pa[:, :]), reads=[pa], awrites=[AB])
            ct = Rot([P.sbuf(name + "ct%d" % i, [128, 32, 128], BF16) for i in range(3)])
            stt = Rot([P.sbuf(name + "sn%d" % i, [128, 32, 128], BF16) for i in range(3)])
            yb = Rot([P.sbuf(name + "yb%d" % i, [128, 512], BF16) for i in range(2)])
            yT = Rot([P.sbuf(name + "yT%d" % i, [128, 4, 128], BF16) for i in range(2)])
            xr = Rot([P.sbuf(name + "xr%d" % i, [128, D], F32) for i in range(2)])
            ot = Rot([P.sbuf(name + "ot%d" % i, [128, D], F32) for i in range(2)])
            yt = Rot([P.sbuf(name + "yt%d" % i, [128, 512], F32) for i in range(2)])
            psy = Rot(self.ps[4:6])
            pso = Rot(self.ps[6:8])
            rsrc = src if half == 0 else dst
            psq = Rot(self.ps[2:6])
            pf = Rot([P.sbuf(name + "pf%d" % i, [128, 512], F32) for i in range(2)])
            qf = Rot([P.sbuf(name + "qf%d" % i, [128, 512], F32) for i in range(2)])
            yd = Rot([P.sbuf(name + "yd%d" % i, [128, 512], BF16) for i in range(2)])
            identJ = rev_last(self.ident_b[:, :])

            def head(kb):
                c = ct.get()
                sn = stt.get()
                P.dma("sync", c[:, :, :], self.c_dftn.ap()[0, kb], reads=[self.c_dftn], writes=[c], sembuf=c)
                P.dma("sync", sn[:, :, :], self.c_dftn.ap()[1, kb], reads=[self.c_dftn], writes=[sn], sembuf=sn)
                pp, pq = psq.get(), psq.get()
                for (tt, ab, pacc) in ((c, 0, pp), (sn, 1, pq)):
                    for nb in range(32):
                        self.mmul(pacc[:, :], tt[:, nb, :], AB[:, nb, ab, :], nb == 0, nb == 31, [tt, AB], pacc, nb == 0)
                p_, q_ = pf.get(), qf.get()
                self.act(p_[:, :], pp[:, :], AF.Copy, [pp], [p_])
                self.act(q_[:, :], pq[:, :], AF.Copy, [pq], [q_])
                y = yb.get()
                self.tt("vector", y[:, :], p_[:, :], q_[:, :], ALU.add, [p_, q_], [y])
                ydf = None
                if kb < 16:
                    ydf = yd.get()
                    self.tt("gpsimd", ydf[:, :], p_[:, :], q_[:, :], ALU.subtract, [p_, q_], [ydf])
                return y, ydf

            def tail(y, t0, m, mirrored):
                pt = pstr.get()
                for cc in range(4):
                    self.mmul(pt[:, cc * 128:(cc + 1) * 128], y[:, cc * 128:(cc + 1) * 128], identJ if mirrored else self.ident_b[:, :], True, True,
                              [y, self.ident_b], pt, cc == 0)
                yt_ = yT.get()
                self.cp("vector", yt_[:, :, :], pt[:, :].rearrange("p (a b) -> p a b", a=4), [pt], [yt_])
                x_r = xr.get()
                P.dma("sync", x_r[0:m, :], rsrc.ap()[t0:t0 + m, :], reads=[rsrc], writes=[x_r], sembuf=x_r)
                o = ot.get()
                for dh in range(2):
                    po = pso.get()
                    for kc in range(4):
                        self.mmul(po[0:m, :], yt_[:, kc, 0:m], wf[:, kc, dh * 512:(dh + 1) * 512], kc == 0, kc == 3, [yt_, wf], po, kc == 0)
                    t = yt.get()
                    self.tt("vector", t[0:m, :], po[0:m, :], G[0:m, dh * 512:(dh + 1) * 512], ALU.mult, [po, G], [t])
                    self.tt("gpsimd", o[0:m, dh * 512:(dh + 1) * 512], t[0:m, :], x_r[0:m, dh * 512:(dh + 1) * 512], ALU.add, [t, x_r],
                            writes=[o] if dh == 0 else [], awrites=[] if dh == 0 else [o])
                P.dma(STQ, dst.ap()[t0:t0 + m, :], o[0:m, :], reads=[o], awrites=[dst], sembuf=o)

            def tails(kb, y, ydf):
                if kb < 16:
                    tail(y, kb * 128, 128, False)
                    tail(ydf, 128 * (31 - kb) + 1, 127 if kb == 0 else 128, True)
                else:
                    tail(y, 2048, 1, False)

            prev = None
            for kb in range(17):
                y, ydf = head(kb)
                if prev is not None:
                    tails(*prev)
                prev = (kb, y, ydf)
            tails(*prev)
            P.pop()

    def build(self):
        self.declare()
        self.setup()
        if self.start_at != "l1":
            self.convert_ffn(0, 0)
        else:
            self.convert_ffn(1, 0)
        self.ada_phase()
        if self.start_at == "l1":
            self.convert_ffn(1, 1)
        cur = self.x
        if self.start_at != "l1":
            self.convert_ffn(0, 1)
            self.ffn_phase("f0", 0, 0, 0, [(self.x, self.xs[0], NT, 0), (self.ctx, self.ctxs, NCTX, 1)])
            self.convert_ffn(1, 0)
            self.convert_ffn(1, 1)
            if self.stop_after == "ffn0":
                return self.finish([self.xs[0], self.ctxs])
            self.mixer_phase("mx", self.xs[0], self.ctxs, self.xs[1])
            if self.stop_after == "mix":
                return self.finish([self.xs[1]])
            self.ffn_phase("f1", 0, 1, 2, [(self.xs[1], self.xs[2], NT, 0)])
            if self.stop_after == "l0":
                return self.finish([self.xs[2]])
            cur = self.xs[2]
        self.ffn_phase("f2", 1, 0, 0, [(cur, self.xs[3], NT, 0)])
        if self.stop_after == "ffn2":
            return self.finish([self.xs[3]])
        self.fourier_phase("fo", 1, 1, self.xs[3], self.xs[4])
        if self.stop_after == "four":
            return self.finish([self.xs[4]])
        self.ffn_phase("f3", 1, 1, 2, [(self.xs[4], self.out, NT, 0)])
        return self.finish([self.out])

    def finish(self, bufs):
        P = self.P
        if self.stop_after is not None:
            for i, b in enumerate(bufs):
                shape = [int(s) for s in b.ap().shape]
                o = self.dout("dbg%d" % i, shape)
                P.dma("sync", o.ap(), b.ap(), reads=[b], writes=[o], sembuf=o)
        P.final_wait("sync", list(self.outputs.values()))
        P.emit()
        return self.nc


def host_constants(need_dft=True):
    c = {}
    c["c_ident"] = np.eye(128, dtype=np.float32)
    rows = NT // GRID_W
    row = np.repeat(np.arange(rows, dtype=np.float32), GRID_W)
    col = np.tile(np.arange(GRID_W, dtype=np.float32), rows)
    n_freq = HD // 4
    inv_freq = (1.0 / (10000.0 ** (np.arange(n_freq, dtype=np.float32) / n_freq))).astype(np.float32)
    ang = np.concatenate([row[:, None] * inv_freq, col[:, None] * inv_freq], axis=-1).astype(np.float32)
    c["c_rope"] = np.concatenate([np.cos(ang), np.sin(ang)], axis=-1).astype(np.float32)
    i = np.arange(128)[:, None]
    r = np.arange(128)[None, :]
    c["c_mask"] = np.stack([(i >= r), (i <= r)]).astype(np.float32)
    c["c_iota"] = np.arange(1, 513, dtype=np.float32)[None, :]
    k = np.arange(256, dtype=np.float64)
    a = 2.0 * np.pi * np.outer(k, k) / 256.0
    c["c_dftc"] = (np.stack([np.cos(a), np.sin(a)]) / 16.0).astype(np.float32)
    if not need_dft:
        return c
    n = np.arange(NT, dtype=np.int64)
    nk = (np.outer(n, n) % NT).astype(np.float64)
    an = 2.0 * np.pi * nk / NT
    import ml_dtypes
    t = (np.stack([np.cos(an), -np.sin(an)]) / 64.0).astype(np.float32).astype(ml_dtypes.bfloat16)
    t = t.reshape(2, 32, 128, 32, 128)
    c["c_dftn"] = np.ascontiguousarray(t.transpose(0, 3, 2, 1, 4))
    return c


_CACHE = {}


def make_in_maps(inputs, b_list, need_dft=True):
    consts = _CACHE.get(("consts", need_dft))
    if consts is None:
        consts = host_constants(need_dft)
        _CACHE[("consts", need_dft)] = consts
    f = lambda a: np.ascontiguousarray(np.asarray(a, dtype=np.float32))
    shared = {
        "w_ada": f(inputs["w_ada"]), "b_ada": f(inputs["b_ada"]), "norm_gain": f(inputs["norm_gain"]),
        "ffn_w1": f(inputs["ffn_w1"]), "ffn_w3": f(inputs["ffn_w3"]), "ffn_w2": f(inputs["ffn_w2"]),
        "w_in": f(inputs["w_in"][0]), "q_gain": f(inputs["q_gain"]), "k_gain": f(inputs["k_gain"]),
        "sink_logit": f(inputs["sink_logit"]),
        "ssm_lam_re": f(inputs["ssm_lam_re"][0]), "ssm_lam_im": f(inputs["ssm_lam_im"][0]), "ssm_log_dt": f(inputs["ssm_log_dt"][0]),
        "ssm_b_re": f(inputs["ssm_b_re"][0]), "ssm_b_im": f(inputs["ssm_b_im"][0]),
        "ssm_c_re": f(inputs["ssm_c_re"][0]), "ssm_c_im": f(inputs["ssm_c_im"][0]),
        "ssm_d": f(inputs["ssm_d"]), "ssm_w_glu": f(inputs["ssm_w_glu"][0]), "w_out": f(inputs["w_out"][0]),
        "fourier_w_out": f(inputs["fourier_w_out"][0]),
    }
    shared.update(consts)
    maps = []
    x = np.asarray(inputs["x"], dtype=np.float32)
    ctx = np.asarray(inputs["ctx"], dtype=np.float32)
    c = np.asarray(inputs["c"], dtype=np.float32)
    c_ctx = np.asarray(inputs["c_ctx"], dtype=np.float32)
    for b in b_list:
        m = dict(shared)
        m["x"] = np.ascontiguousarray(x[b])
        m["ctx"] = np.ascontiguousarray(ctx[b])
        m["cvec"] = np.ascontiguousarray(np.stack([c[b], c_ctx]))
        maps.append(m)
    return maps


def kernel(**inputs):
    k = K()
    nc = k.build()
    in_maps = make_in_maps(inputs, list(range(8)), need_dft=("c_dftn" in k.inputs))
    in_maps = [{n: m[n] for n in k.inputs} for m in in_maps]
    res = run_bass_kernel_spmd(nc, in_maps, core_ids=list(range(8)))
    return np.stack([np.asarray(r["out"], dtype=np.float32) for r in res.results], axis=0)
```

```python
import math
import os
import numpy as np
import concourse.bass as bass
import concourse.mybir as mybir
from concourse.bass_utils import run_bass_kernel_spmd

F32 = mybir.dt.float32
BF16 = mybir.dt.bfloat16
AF = mybir.ActivationFunctionType
ALU = mybir.AluOpType
AX = mybir.AxisListType

D = 1024
NT = 4096
NCTX = 256
FF = 2816
NFG = 11
EPS = 1e-6
HD = 64
NQH = 8
NKV = 2
GRID_W = 64
TWO_PI = 2.0 * math.pi
MAGIC = 12582912.0
CW1 = 6.28125
CW2 = TWO_PI - CW1

SAME_ENGINE_SYNC = True
ADA_BF16 = False
STQ = "gpsimd"


class Buf:
    def __init__(self, h, name):
        self.h = h
        self.name = name
        self.w = {}
        self.wacc = {}
        self.r = {}
        self.sem = None
        self.sem_sw = False
        self.cnt = 0
        self.excl = False
        self.inh = {}

    def __getitem__(self, idx):
        return self.h[idx]

    def ap(self):
        return self.h.ap()


class Eng:
    def __init__(self, name, sem, is_pe=False):
        self.name = name
        self.sem = sem
        self.count = 0
        self.seen = {}
        self.ops = []
        self.is_pe = is_pe


class Prog:
    def __init__(self, nc):
        self.nc = nc
        self.eng = {}
        for name in ("tensor", "vector", "scalar", "gpsimd", "sync"):
            self.eng[name] = Eng(name, nc.alloc_semaphore("s_" + name), is_pe=(name == "tensor"))
        self.nsem = 5
        self.inherit = {}
        self.scopes = []
        self.free_sems = {False: [], True: []}

    def _reg(self, b):
        b.inh = dict(self.inherit)
        if self.scopes:
            self.scopes[-1][1].append(b)
        return b

    def sbuf(self, name, shape, dtype):
        self.uid = getattr(self, "uid", 0) + 1
        name = "%s_%d" % (name, self.uid)
        if self.scopes:
            cm = self.nc.sbuf_tensor(name, list(shape), dtype)
            h = cm.__enter__()
            self.scopes[-1][0].append(cm)
        else:
            h = self.nc.alloc_sbuf_tensor(name, list(shape), dtype)
        return self._reg(Buf(h, name))

    def psum(self, name, shape, dtype=F32):
        b = Buf(self.nc.alloc_psum_tensor(name, list(shape), dtype), name)
        b.excl = True
        return b

    def dram(self, name, shape, dtype, kind="Internal"):
        return Buf(self.nc.dram_tensor(name, list(shape), dtype, kind=kind), name)

    def push(self):
        self.scopes.append(([], []))

    def pop(self):
        cms, bufs = self.scopes.pop()
        for b in bufs:
            if b.sem is not None:
                self.free_sems[b.sem_sw].append((b.sem, b.cnt))
            for d in (b.w, b.wacc, b.r, b.inh):
                for s, v in d.items():
                    if self.inherit.get(s, 0) < v:
                        self.inherit[s] = v
        for cm in reversed(cms):
            cm.__exit__(None, None, None)

    def _waits(self, E, reads, writes, awrites):
        waits = {}

        def merge(d):
            for s, v in d.items():
                if waits.get(s, 0) < v:
                    waits[s] = v
        for b in reads:
            merge(b.w)
            merge(b.wacc)
            merge(b.inh)
            if b.excl:
                merge(b.r)
        for b in writes:
            merge(b.w)
            merge(b.wacc)
            merge(b.r)
            merge(b.inh)
        for b in awrites:
            merge(b.w)
            merge(b.r)
            merge(b.inh)
        wl = []
        for s, v in waits.items():
            if s is E.sem and (E.is_pe or not SAME_ENGINE_SYNC):
                continue
            if E.seen.get(s, 0) >= v:
                continue
            E.seen[s] = v
            wl.append((s, v))
        return wl

    def _record(self, sem, val, reads, writes, awrites):
        for b in reads:
            if b.r.get(sem, 0) < val:
                b.r[sem] = val
        for b in writes:
            b.w = {sem: val}
            b.wacc = {}
            b.r = {}
        for b in awrites:
            if b.wacc.get(sem, 0) < val:
                b.wacc[sem] = val

    def op(self, e, fn, reads=(), writes=(), awrites=()):
        E = self.eng[e]
        wl = self._waits(E, reads, writes, awrites)
        E.count += 1
        E.ops.append((wl, fn, E.sem, 1))
        self._record(E.sem, E.count, reads, writes, awrites)

    def dma(self, q, out_ap, in_ap, reads=(), writes=(), awrites=(), sembuf=None, **kw):
        E = self.eng[q]
        wl = self._waits(E, reads, writes, awrites)
        if sembuf.sem is None:
            sembuf.sem_sw = (q == "gpsimd")
            if self.free_sems[sembuf.sem_sw]:
                sembuf.sem, sembuf.cnt = self.free_sems[sembuf.sem_sw].pop()
            else:
                sembuf.sem = self.nc.alloc_semaphore("d_" + sembuf.name)
                self.nsem += 1
        sembuf.cnt += 16
        E.ops.append((wl, lambda eng: eng.dma_start(out=out_ap, in_=in_ap, **kw), sembuf.sem, 16))
        self._record(sembuf.sem, sembuf.cnt, reads, writes, awrites)

    def final_wait(self, q, bufs):
        E = self.eng[q]
        wl = self._waits(E, (), bufs, ())
        E.ops.append((wl, None, None, 0))

    def emit(self):
        nc = self.nc

        def run(E):
            def f(eng):
                for wl, fn, sem, inc in E.ops:
                    for s, v in wl:
                        eng.wait_ge(s, v)
                    if fn is not None:
                        fn(eng).then_inc(sem, inc)
            return f
        with nc.Block() as block:
            block.sync(run(self.eng["sync"]))
            block.scalar(run(self.eng["scalar"]))
            block.vector(run(self.eng["vector"]))
            block.gpsimd(run(self.eng["gpsimd"]))
            block.tensor(run(self.eng["tensor"]))


class Rot:
    def __init__(self, bufs):
        self.bufs = bufs
        self.i = 0

    def get(self):
        b = self.bufs[self.i % len(self.bufs)]
        self.i += 1
        return b


def sub_last(a, start, step, count):
    dims = [list(d) for d in a.ap]
    st, n = dims[-1]
    dims[-1] = [st * step, count]
    return bass.AP(a.tensor, a.offset + st * start, dims)


def rev_last(a):
    dims = [list(d) for d in a.ap]
    st, n = dims[-1]
    dims[-1] = [-st, n]
    return bass.AP(a.tensor, a.offset + st * (n - 1), dims)


class K:
    def __init__(self, taps=(), stop_after=None, start_at=None):
        self.taps = set(taps)
        self.stop_after = stop_after
        self.start_at = start_at
        self.skip_ssm = False
        self.nc = bass.Bass("TRN2", target_bir_lowering=False)
        self.P = Prog(self.nc)
        self.inputs = {}
        self.outputs = {}

    def din(self, name, shape, dtype=F32):
        b = self.P.dram(name, shape, dtype, kind="ExternalInput")
        self.inputs[name] = b
        return b

    def dout(self, name, shape, dtype=F32):
        b = self.P.dram(name, shape, dtype, kind="ExternalOutput")
        self.outputs[name] = b
        return b

    def declare(self):
        P = self.P
        self.x = self.din("x", [NT, D])
        self.ctx = self.din("ctx", [NCTX, D])
        self.cvec = self.din("cvec", [2, D])
        self.w_ada = self.din("w_ada", [2, D, 9 * D])
        self.b_ada = self.din("b_ada", [2, 9 * D])
        self.norm_gain = self.din("norm_gain", [2, 3, D])
        self.ffn_w1 = self.din("ffn_w1", [2, 2, D, FF])
        self.ffn_w3 = self.din("ffn_w3", [2, 2, D, FF])
        self.ffn_w2 = self.din("ffn_w2", [2, 2, FF, D])
        self.w_in = self.din("w_in", [D, 1280])
        self.q_gain = self.din("q_gain", [1, HD])
        self.k_gain = self.din("k_gain", [1, HD])
        self.sink = self.din("sink_logit", [1, NQH])
        self.lam_re = self.din("ssm_lam_re", [2, 32, 64])
        self.lam_im = self.din("ssm_lam_im", [2, 32, 64])
        self.log_dt = self.din("ssm_log_dt", [2, 32])
        self.b_re = self.din("ssm_b_re", [2, 32, 64, 16])
        self.b_im = self.din("ssm_b_im", [2, 32, 64, 16])
        self.c_re = self.din("ssm_c_re", [2, 32, 16, 64])
        self.c_im = self.din("ssm_c_im", [2, 32, 16, 64])
        self.ssm_d = self.din("ssm_d", [1, 512])
        self.w_glu = self.din("ssm_w_glu", [512, 512])
        self.w_out = self.din("w_out", [D, D])
        self.w_f = self.din("fourier_w_out", [D, D])
        self.c_ident = self.din("c_ident", [128, 128])
        self.c_rope = self.din("c_rope", [NT, 64])
        self.c_mask = self.din("c_mask", [2, 128, 128])
        self.c_iota = self.din("c_iota", [1, 512])
        self.c_dftc = self.din("c_dftc", [2, 256, 256])
        self.out = self.dout("out", [NT, D])
        self.mrow = P.dram("mrow", [2, 2, 9 * D], F32)
        self.xs = [P.dram("xs%d" % i, [NT, D], F32) for i in range(5)]
        self.ctxs = P.dram("ctxs", [NCTX, D], F32)
        self.w1s = [[P.dram("w1s%d%d" % (l, h), [NFG, 128, 8, 256], BF16) for h in range(2)] for l in range(2)]
        self.w3s = [[P.dram("w3s%d%d" % (l, h), [NFG, 128, 8, 256], BF16) for h in range(2)] for l in range(2)]
        self.w2s = [[P.dram("w2s%d%d" % (l, h), [NFG, 128, 2, D], BF16) for h in range(2)] for l in range(2)]

    def tap(self, name, buf_ap, shape, reads, dtype=F32):
        if name not in self.taps:
            return
        o = self.dout("tap_" + name, shape, dtype)
        self.P.dma("sync", o.ap(), buf_ap, reads=reads, writes=[o], sembuf=reads[0])

    def setup(self):
        P = self.P
        self.ident_f = P.sbuf("ident_f", [128, 128], F32)
        self.ident_b = P.sbuf("ident_b", [128, 128], BF16)
        self.epsc = P.sbuf("epsc", [128, 1], F32)
        P.dma("sync", self.ident_f[:, :], self.c_ident.ap(), reads=[self.c_ident], writes=[self.ident_f], sembuf=self.ident_f)
        P.op("vector", lambda e: e.tensor_copy(out=self.ident_b[:, :], in_=self.ident_f[:, :]), reads=[self.ident_f], writes=[self.ident_b])
        P.op("vector", lambda e: e.memset(self.epsc[:, :], EPS), writes=[self.epsc])
        self.ps = [P.psum("ps%d" % i, [128, 512], F32) for i in range(8)]

    def convert_ffn(self, l, h):
        P = self.P
        w1 = self.ffn_w1.ap()[l, h].rearrange("(kc p) (fg f) -> fg p kc f", p=128, f=256)
        w3 = self.ffn_w3.ap()[l, h].rearrange("(kc p) (fg f) -> fg p kc f", p=128, f=256)
        w2 = self.ffn_w2.ap()[l, h].rearrange("(fg fc p) d -> fg p fc d", p=128, fc=2)
        for fg in range(NFG):
            P.dma("gpsimd", self.w1s[l][h][fg], w1[fg], reads=[self.ffn_w1], awrites=[self.w1s[l][h]], sembuf=self.w1s[l][h])
            P.dma("gpsimd", self.w3s[l][h][fg], w3[fg], reads=[self.ffn_w3], awrites=[self.w3s[l][h]], sembuf=self.w3s[l][h])
        for fg in range(NFG):
            P.dma("gpsimd", self.w2s[l][h][fg], w2[fg], reads=[self.ffn_w2], awrites=[self.w2s[l][h]], sembuf=self.w2s[l][h])

    def ada_phase(self):
        P = self.P
        P.push()
        ctmp = P.sbuf("ctmp", [128, 2, 8], F32)
        ADA_DT = BF16 if ADA_BF16 else F32
        cc2 = P.sbuf("cc2", [128, 8, 128], ADA_DT)
        P.op("vector", lambda e: e.memset(cc2[:, :, :], 0.0), writes=[cc2])
        for v in range(2):
            P.dma("sync", ctmp[:, v, :], self.cvec.ap()[v].rearrange("(p k) -> p k", k=8), reads=[self.cvec],
                  awrites=[ctmp], sembuf=ctmp)
        for v in range(2):
            P.op("scalar", lambda e, v=v: e.activation(out=cc2[:, :, v], in_=ctmp[:, v, :], func=AF.Silu),
                 reads=[ctmp], awrites=[cc2])
        was = Rot([P.sbuf("wa%d" % i, [128, 8, 512], ADA_DT) for i in range(4 if ADA_BF16 else 3)])
        bada = P.sbuf("bada", [2, 9 * D], F32)
        mr = P.sbuf("mr", [2, 9 * D], F32)
        pss = Rot(self.ps[0:2])
        for l in range(2):
            P.dma("sync", bada[:, :], self.b_ada.ap()[l:l + 1, :].partition_broadcast(2), reads=[self.b_ada], writes=[bada], sembuf=bada)
            wsrc = self.w_ada.ap()[l].rearrange("(p k) n -> p k n", k=8)
            for nt in range(18):
                wa = was.get()
                P.dma("gpsimd" if ADA_BF16 else "sync", wa[:, :, :], wsrc[:, :, nt * 512:(nt + 1) * 512], reads=[self.w_ada], writes=[wa], sembuf=wa)
                ps = pss.get()
                for kc in range(8):
                    P.op("tensor", lambda e, ps=ps, wa=wa, kc=kc: e.matmul(ps[:, :], lhsT=cc2[:, kc, :],
                                                                       rhs=wa[:, kc, :],
                                                                       start=(kc == 0), stop=(kc == 7)),
                         reads=[cc2, wa], writes=[ps] if kc == 0 else [], awrites=[] if kc == 0 else [ps])
                P.op("vector", lambda e, ps=ps, nt=nt: e.tensor_tensor(out=mr[:, nt * 512:(nt + 1) * 512], in0=ps[0:2, :],
                                                                      in1=bada[:, nt * 512:(nt + 1) * 512], op=ALU.add),
                     reads=[ps, bada], awrites=[mr])
            P.dma("scalar", self.mrow.ap()[l], mr[:, :], reads=[mr], writes=[] , awrites=[self.mrow], sembuf=mr)
            if "mrow" in self.taps:
                o_ = self.dout("tap_mrow%d" % l, [2, 9 * D], F32)
                P.dma("sync", o_.ap(), mr[:, :], reads=[mr], writes=[o_], sembuf=mr)
        P.pop()

    def load_rows(self, pref, l, v, site, gmul, need_g=True):
        P = self.P
        S = P.sbuf(pref + "S", [128, D], F32)
        Sh = P.sbuf(pref + "Sh", [128, D], F32)
        m = self.mrow.ap()[l, v]
        base = 3 * site * D
        P.dma("sync", Sh[:, :], m[base:base + D].partition_broadcast(128), reads=[self.mrow], writes=[Sh], sembuf=Sh)
        P.dma("sync", S[:, :], m[base + D:base + 2 * D].partition_broadcast(128), reads=[self.mrow], writes=[S], sembuf=S)
        G = None
        if need_g:
            G = P.sbuf(pref + "G", [128, D], F32)
            P.dma("sync", G[:, :], m[base + 2 * D:base + 3 * D].partition_broadcast(128), reads=[self.mrow], writes=[G], sembuf=G)
        gn = P.sbuf(pref + "gn", [128, D], F32)
        P.dma("sync", gn[:, :], self.norm_gain.ap()[l, site].partition_broadcast(128), reads=[self.norm_gain], writes=[gn], sembuf=gn)
        P.op("vector", lambda e: e.scalar_tensor_tensor(out=S[:, :], in0=S[:, :], scalar=1.0, in1=gn[:, :], op0=ALU.add, op1=ALU.mult),
             reads=[S, gn], writes=[S])
        if need_g and gmul != 1.0:
            P.op("vector", lambda e: e.tensor_scalar(out=G[:, :], in0=G[:, :], scalar1=gmul, scalar2=None, op0=ALU.mult),
                 reads=[G], writes=[G])
        return S, Sh, G

    def make_s1_bufs(self, pref, small=False):
        P = self.P
        b = {}
        b["xt"] = Rot([P.sbuf(pref + "xt%d" % i, [128, D], F32) for i in range(2 if small else 5)])
        b["junk"] = P.sbuf(pref + "junk", [128, D], BF16)
        b["t1"] = Rot([P.sbuf(pref + "t1%d" % i, [128, D], F32) for i in range(1 if small else 2)])
        b["xm"] = Rot([P.sbuf(pref + "xm%d" % i, [128, D], BF16) for i in range(2 if small else 4)])
        b["st"] = Rot([P.sbuf(pref + "st%d" % i, [128, 4], F32) for i in range(4)])
        return b

    def stage1(self, b, src_buf, src_ap, S, Sh, xnT, col0, pstr):
        xm = self.stage1a(b, src_buf, src_ap, S, Sh)
        self.stage1b(xm, xnT, col0, pstr)

    def stage1a(self, b, src_buf, src_ap, S, Sh):
        P = self.P
        xt = b["xt"].get()
        P.dma("sync", xt[:, :], src_ap, reads=[src_buf], writes=[xt], sembuf=xt)
        st = b["st"].get()
        junk = b["junk"]
        P.op("scalar", lambda e: e.activation(out=junk[:, :], in_=xt[:, :], func=AF.Square, accum_out=st[:, 0:1]),
             reads=[xt], writes=[junk, st])
        P.op("scalar", lambda e: e.activation(out=st[:, 1:2], in_=st[:, 0:1], func=AF.Sqrt, bias=self.epsc[:, 0:1], scale=1.0 / D),
             reads=[st, self.epsc], writes=[st])
        P.op("vector", lambda e: e.reciprocal(out=st[:, 2:3], in_=st[:, 1:2]), reads=[st], writes=[st])
        t1 = b["t1"].get()
        P.op("vector", lambda e: e.scalar_tensor_tensor(out=t1[:, :], in0=xt[:, :], scalar=st[:, 2:3], in1=S[:, :], op0=ALU.mult, op1=ALU.mult),
             reads=[xt, st, S], writes=[t1])
        xm = b["xm"].get()
        P.op("gpsimd", lambda e: e.tensor_tensor(out=xm[:, :], in0=t1[:, :], in1=Sh[:, :], op=ALU.add), reads=[t1, Sh], writes=[xm])
        return xm

    def stage1b(self, xm, xnT, col0, pstr):
        P = self.P
        for half in range(2):
            ps = pstr.get()
            for c in range(4):
                kc = half * 4 + c
                P.op("tensor", lambda e, ps=ps, c=c, kc=kc: e.matmul(ps[:, c * 128:(c + 1) * 128], lhsT=xm[:, kc * 128:(kc + 1) * 128],
                                                                    rhs=self.ident_b[:, :], start=True, stop=True),
                     reads=[xm, self.ident_b], writes=[ps] if c == 0 else [], awrites=[] if c == 0 else [ps])
            dst = xnT[:, half * 4:half * 4 + 4, col0:col0 + 128]
            src = ps[:, :].rearrange("p (a b) -> p a b", a=4)
            if half == 0:
                P.op("scalar", lambda e, dst=dst, src=src: e.activation(out=dst, in_=src, func=AF.Copy), reads=[ps], awrites=[xnT])
            else:
                P.op("vector", lambda e, dst=dst, src=src: e.tensor_copy(out=dst, in_=src), reads=[ps], awrites=[xnT])

    def ffn_phase(self, name, l, h, site, jobs):
        P = self.P
        P.push()
        w2r = P.sbuf(name + "w2r", [128, 22, D], BF16)
        for fg in range(NFG):
            P.dma("sync", w2r[:, 2 * fg:2 * fg + 2, :], self.w2s[l][h][fg], reads=[self.w2s[l][h]], awrites=[w2r], sembuf=w2r)
        rows = {}
        for v in sorted(set(j[3] for j in jobs)):
            rows[v] = self.load_rows(name + "r%d" % v, l, v, site, 0.5)
        s1b = self.make_s1_bufs(name)
        xnTs = Rot([P.sbuf(name + "xnT%d" % i, [128, 8, 512], BF16) for i in range(2)])
        g = P.sbuf(name + "g", [128, 22, 512], BF16)
        w1t = Rot([P.sbuf(name + "w1t%d" % i, [128, 8, 256], BF16) for i in range(3)])
        w3t = Rot([P.sbuf(name + "w3t%d" % i, [128, 8, 256], BF16) for i in range(3)])
        sl = Rot([P.sbuf(name + "sl%d" % i, [128, 512], F32) for i in range(2)])
        yt = Rot([P.sbuf(name + "yt%d" % i, [128, 512], F32) for i in range(2)])
        xr = Rot([P.sbuf(name + "xr%d" % i, [128, D], F32) for i in range(2)])
        ot = Rot([P.sbuf(name + "ot%d" % i, [128, D], F32) for i in range(2)])
        pstr = Rot(self.ps[0:2])
        psh = Rot(self.ps[2:6])
        psy = Rot(self.ps[6:8])
        tiles = []
        for (src, dst, ntok, v) in jobs:
            for t0 in range(0, ntok, 512):
                tiles.append((src, dst, t0, min(512, ntok - t0), v))

        def S1a(tile):
            src, dst, t0, n, v = tile
            S, Sh, G = rows[v]
            return [self.stage1a(s1b, src, src.ap()[t0 + j * 128:t0 + (j + 1) * 128, :], S, Sh) for j in range(n // 128)]

        def S1b(xms):
            xnT = xnTs.get()
            for j, xm in enumerate(xms):
                self.stage1b(xm, xnT, j * 128, pstr)
            return xnT

        def S2(tile, xnT):
            n = tile[3]
            for fg in range(NFG):
                a = w1t.get()
                b3 = w3t.get()
                P.dma("sync", a[:, :, :], self.w1s[l][h][fg], reads=[self.w1s[l][h]], writes=[a], sembuf=a)
                P.dma("sync", b3[:, :, :], self.w3s[l][h][fg], reads=[self.w3s[l][h]], writes=[b3], sembuf=b3)
                for fc in range(2):
                    p1 = psh.get()
                    p3 = psh.get()
                    for (pp, wt) in ((p1, a), (p3, b3)):
                        for kc in range(8):
                            P.op("tensor", lambda e, pp=pp, wt=wt, kc=kc, fc=fc: e.matmul(pp[:, 0:n], lhsT=wt[:, kc, fc * 128:(fc + 1) * 128],
                                                                                       rhs=xnT[:, kc, 0:n], start=(kc == 0), stop=(kc == 7)),
                                 reads=[wt, xnT], writes=[pp] if kc == 0 else [], awrites=[] if kc == 0 else [pp])
                    s = sl.get()
                    P.op("scalar", lambda e, s=s, p1=p1: e.activation(out=s[:, 0:n], in_=p1[:, 0:n], func=AF.Silu), reads=[p1], writes=[s])
                    fi = fg * 2 + fc
                    P.op("vector", lambda e, s=s, p3=p3, fi=fi: e.tensor_tensor(out=g[:, fi, 0:n], in0=s[:, 0:n], in1=p3[:, 0:n], op=ALU.mult),
                         reads=[s, p3], awrites=[g])

        def S3(tile):
            src, dst, t0, n, v = tile
            S, Sh, G = rows[v]
            for j in range(n // 128):
                x_r = xr.get()
                P.dma("sync", x_r[:, :], src.ap()[t0 + j * 128:t0 + (j + 1) * 128, :], reads=[src], writes=[x_r], sembuf=x_r)
                o = ot.get()
                for dh in range(2):
                    py = psy.get()
                    for fi in range(22):
                        P.op("tensor", lambda e, py=py, fi=fi, j=j, dh=dh: e.matmul(py[:, :], lhsT=g[:, fi, j * 128:(j + 1) * 128],
                                                                                 rhs=w2r[:, fi, dh * 512:(dh + 1) * 512], start=(fi == 0), stop=(fi == 21)),
                             reads=[g, w2r], writes=[py] if fi == 0 else [], awrites=[] if fi == 0 else [py])
                    y = yt.get()
                    P.op("vector", lambda e, y=y, py=py, dh=dh: e.tensor_tensor(out=y[:, :], in0=py[:, :], in1=G[:, dh * 512:(dh + 1) * 512], op=ALU.mult),
                         reads=[py, G], writes=[y])
                    P.op("gpsimd", lambda e, y=y, o=o, x_r=x_r, dh=dh: e.tensor_tensor(out=o[:, dh * 512:(dh + 1) * 512], in0=y[:, :],
                                                                                   in1=x_r[:, dh * 512:(dh + 1) * 512], op=ALU.add),
                         reads=[y, x_r], writes=[o] if dh == 0 else [], awrites=[] if dh == 0 else [o])
                P.dma(STQ, dst.ap()[t0 + j * 128:t0 + (j + 1) * 128, :], o[:, :], reads=[o], awrites=[dst], sembuf=o)

        cur = S1b(S1a(tiles[0]))
        for i, tile in enumerate(tiles):
            S2(tile, cur)
            xms = S1a(tiles[i + 1]) if i + 1 < len(tiles) else None
            S3(tile)
            cur = S1b(xms) if xms is not None else None
        P.pop()


    def mixer_phase(self, name, src, csrc, dst):
        P = self.P
        l, site = 0, 1
        P.push()
        attnT = P.sbuf(name + "attnT", [128, 4, NT], BF16)
        uT = [P.sbuf(name + "uT%d" % i, [128, 4, 256 if i == 0 else 512], BF16) for i in range(9)]
        self.mix_attn(name, src, csrc, attnT, uT)
        if "attnT" in self.taps:
            o = self.dout("tap_attnT", [128, 4, NT], BF16)
            P.dma("sync", o.ap(), attnT[:, :, :], reads=[attnT], writes=[o], sembuf=attnT)
        if "uT" in self.taps:
            o = self.dout("tap_uT", [128, 4, 256 + NT], BF16)
            for i in range(9):
                c0 = 0 if i == 0 else 256 + (i - 1) * 512
                P.dma("sync", o.ap()[:, :, c0:c0 + (256 if i == 0 else 512)], uT[i][:, :, :], reads=[uT[i]], awrites=[o], sembuf=uT[i])
        if not self.skip_ssm:
            self.mix_ssm(name, uT)
        if "ssmT" in self.taps:
            o = self.dout("tap_ssmT", [128, 4, NT], BF16)
            for i in range(1, 9):
                P.dma("sync", o.ap()[:, :, (i - 1) * 512:i * 512], uT[i][:, :, :], reads=[uT[i]], awrites=[o], sembuf=uT[i])
        self.mix_out(name, src, dst, attnT, uT)
        P.pop()

    def mix_attn(self, name, src, csrc, attnT, uT):
        P = self.P
        l, site = 0, 1
        P.push()
        kT = P.sbuf(name + "kT", [128, NCTX + NT], BF16)
        V = P.sbuf(name + "V", [128, 34, 2, 65], BF16)
        P.op("vector", lambda e: e.memset(V[:, :, :, :], 1.0), writes=[V])
        P.push()
        win = P.sbuf(name + "win", [128, 8, 1280], BF16)
        P.dma("gpsimd", win[:, :, :], self.w_in.ap().rearrange("(k p) n -> p k n", p=128), reads=[self.w_in], writes=[win], sembuf=win)
        rows = {}
        for v in range(2):
            rows[v] = self.load_rows(name + "r%d" % v, l, v, site, 1.0, need_g=False)
        s1b = self.make_s1_bufs(name, small=True)
        hTs = Rot([P.sbuf(name + "hT%d" % i, [128, 8, 128], BF16) for i in range(2)])
        gq = P.sbuf(name + "gq", [128, 10, 64], F32)
        P.dma("sync", gq[:, 0, :], self.q_gain.ap()[0:1, :].partition_broadcast(128), reads=[self.q_gain], awrites=[gq], sembuf=gq)
        P.dma("sync", gq[:, 8, :], self.k_gain.ap()[0:1, :].partition_broadcast(128), reads=[self.k_gain], awrites=[gq], sembuf=gq)
        P.op("vector", lambda e: e.tensor_scalar(out=gq[:, 0, :], in0=gq[:, 0, :], scalar1=0.125, scalar2=None, op0=ALU.mult), reads=[gq], awrites=[gq])
        for hh in range(1, 8):
            P.op("vector", lambda e, hh=hh: e.tensor_copy(out=gq[:, hh, :], in_=gq[:, 0, :]), reads=[gq], awrites=[gq])
        P.op("vector", lambda e: e.tensor_copy(out=gq[:, 9, :], in_=gq[:, 8, :]), reads=[gq], awrites=[gq])
        esink = P.sbuf(name + "esink", [128, 8], F32)
        P.dma("sync", esink[:, :], self.sink.ap()[0:1, :].partition_broadcast(128), reads=[self.sink], writes=[esink], sembuf=esink)
        P.op("scalar", lambda e: e.activation(out=esink[:, :], in_=esink[:, :], func=AF.Exp), reads=[esink], writes=[esink])
        mk = P.sbuf(name + "mk", [128, 2, 128], BF16)
        P.dma("gpsimd", mk[:, :, :], self.c_mask.ap().rearrange("m i r -> i m r"), reads=[self.c_mask], writes=[mk], sembuf=mk)
        qk = Rot([P.sbuf(name + "qk%d" % i, [128, 10, 64], F32) for i in range(2)])
        sq = P.sbuf(name + "sq", [128, 10, 64], F32)
        qn = Rot([P.sbuf(name + "qn%d" % i, [128, 10, 64], F32) for i in range(2)])
        ta = P.sbuf(name + "ta", [128, 10, 32], F32)
        tb = P.sbuf(name + "tb", [128, 10, 32], F32)
        tc = P.sbuf(name + "tc", [128, 10, 32], F32)
        td = P.sbuf(name + "td", [128, 10, 32], F32)
        qkr = Rot([P.sbuf(name + "qkr%d" % i, [128, 10, 64], BF16) for i in range(2)])
        rp = Rot([P.sbuf(name + "rp%d" % i, [128, 64], F32) for i in range(2)])
        sm = Rot([P.sbuf(name + "sm%d" % i, [128, 32], F32) for i in range(4)])
        ub = Rot([P.sbuf(name + "ub%d" % i, [128, 512], BF16) for i in range(2)])
        qTs = Rot([P.sbuf(name + "qT%d" % i, [128, 4, 128], BF16) for i in range(3)])
        pTs = Rot([P.sbuf(name + "pT%d" % i, [128, 5, 512], BF16) for i in range(2)])
        Ot = Rot([P.sbuf(name + "Ot%d" % i, [128, 8, 64], BF16) for i in range(2)])
        psr = Rot(self.ps)

        def mm(ps_ap, lhsT, rhs, first, last, reads, psb, first_in_bank):
            P.op("tensor", lambda e: e.matmul(ps_ap, lhsT=lhsT, rhs=rhs, start=first, stop=last),
                 reads=reads, writes=[psb] if first_in_bank else [], awrites=[] if first_in_bank else [psb])

        def pre(blk, is_ctx):
            v = 1 if is_ctx else 0
            S, Sh, _ = rows[v]
            sbuf_src = csrc if is_ctx else src
            r0 = blk * 128 if is_ctx else (blk - 2) * 128
            return self.stage1a(s1b, sbuf_src, sbuf_src.ap()[r0:r0 + 128, :], S, Sh)

        def project(blk, is_ctx, xm, res):
            hT = hTs.get()
            r0 = blk * 128 if is_ctx else (blk - 2) * 128
            self.stage1b(xm, hT, 0, psr)
            yield
            h0 = 8 if is_ctx else 0
            nh = 10 - h0
            q_k = qk.get()
            if not is_ctx:
                pq = psr.get()
                for kc in range(8):
                    mm(pq[:, :], hT[:, kc, :], win[:, kc, 0:512], kc == 0, kc == 7, [hT, win], pq, kc == 0)
                P.op("scalar", lambda e, pq=pq, q_k=q_k: e.activation(
                    out=q_k[:, 0:8, :].rearrange("p (a hi) d -> p hi a d", hi=2),
                    in_=pq[:, :].rearrange("p (hi a d) -> p hi a d", hi=2, a=4), func=AF.Copy), reads=[pq], writes=[q_k])
            yield
            pkv = psr.get()
            for kc in range(8):
                mm(pkv[:, 0:256], hT[:, kc, :], win[:, kc, 512:768], kc == 0, kc == 7, [hT, win], pkv, kc == 0)
            P.op("scalar", lambda e, pkv=pkv, q_k=q_k: e.activation(out=q_k[:, 8:10, :], in_=pkv[:, 0:128].rearrange("p (h d) -> p h d", h=2), func=AF.Copy),
                 reads=[pkv], writes=[q_k] if is_ctx else [], awrites=[] if is_ctx else [q_k])
            P.op("vector", lambda e, pkv=pkv, blk=blk: e.tensor_copy(out=V[:, blk, :, 0:64], in_=pkv[:, 128:256].rearrange("p (h d) -> p h d", h=2)),
                 reads=[pkv], awrites=[V])
            yield
            pu = psr.get()
            for kc in range(8):
                mm(pu[:, :], hT[:, kc, :], win[:, kc, 768:1280], kc == 0, kc == 7, [hT, win], pu, kc == 0)
            u_b = ub.get()
            P.op("scalar", lambda e, pu=pu, u_b=u_b: e.activation(out=u_b[:, :], in_=pu[:, :], func=AF.Copy), reads=[pu], writes=[u_b])
            put = psr.get()
            for c in range(4):
                mm(put[:, c * 128:(c + 1) * 128], u_b[:, c * 128:(c + 1) * 128], self.ident_b[:, :], True, True, [u_b, self.ident_b], put, c == 0)
            if is_ctx:
                ut, c0 = uT[0], blk * 128
            else:
                ut, c0 = uT[1 + (blk - 2) // 4], ((blk - 2) % 4) * 128
            P.op("vector", lambda e, put=put, ut=ut, c0=c0: e.tensor_copy(out=ut[:, :, c0:c0 + 128], in_=put[:, :].rearrange("p (a b) -> p a b", a=4)),
                 reads=[put], awrites=[ut])
            yield
            hs = slice(h0, 10)
            sm_ = sm.get()
            P.op("scalar", lambda e: e.activation(out=sq[:, hs, :], in_=q_k[:, hs, :], func=AF.Square), reads=[q_k], writes=[sq])
            P.op("vector", lambda e: e.tensor_reduce(out=sm_[:, h0:10], in_=sq[:, hs, :], axis=AX.X, op=ALU.add), reads=[sq], writes=[sm_])
            P.op("scalar", lambda e: e.activation(out=sm_[:, 10 + h0:20], in_=sm_[:, h0:10], func=AF.Sqrt, bias=self.epsc[:, 0:1], scale=1.0 / HD),
                 reads=[sm_, self.epsc], writes=[sm_])
            P.op("vector", lambda e: e.reciprocal(out=sm_[:, 20 + h0:30], in_=sm_[:, 10 + h0:20]), reads=[sm_], writes=[sm_])
            yield
            q_n = qn.get()
            P.op("vector", lambda e: e.tensor_tensor(out=q_n[:, hs, :], in0=q_k[:, hs, :], in1=sm_[:, 20 + h0:30].unsqueeze(2).to_broadcast([128, nh, 64]), op=ALU.mult),
                 reads=[q_k, sm_], writes=[q_n])
            q_r = qkr.get()
            if is_ctx:
                P.op("gpsimd", lambda e: e.tensor_tensor(out=q_r[:, hs, :], in0=q_n[:, hs, :], in1=gq[:, hs, :], op=ALU.mult), reads=[q_n, gq], writes=[q_r])
            else:
                P.op("gpsimd", lambda e: e.tensor_tensor(out=q_n[:, :, :], in0=q_n[:, :, :], in1=gq[:, :, :], op=ALU.mult), reads=[q_n, gq], writes=[q_n])
                yield
                r_p = rp.get()
                P.dma("sync", r_p[:, :], self.c_rope.ap()[r0:r0 + 128, :], reads=[self.c_rope], writes=[r_p], sembuf=r_p)
                cosb = r_p[:, 0:32].unsqueeze(1).to_broadcast([128, 10, 32])
                sinb = r_p[:, 32:64].unsqueeze(1).to_broadcast([128, 10, 32])
                x1 = q_n[:, :, 0:32]
                x2 = q_n[:, :, 32:64]
                P.op("vector", lambda e: e.tensor_tensor(out=ta[:, :, :], in0=x1, in1=cosb, op=ALU.mult), reads=[q_n, r_p], writes=[ta])
                P.op("gpsimd", lambda e: e.tensor_tensor(out=tb[:, :, :], in0=x2, in1=sinb, op=ALU.mult), reads=[q_n, r_p], writes=[tb])
                P.op("vector", lambda e: e.tensor_tensor(out=q_r[:, :, 0:32], in0=ta[:, :, :], in1=tb[:, :, :], op=ALU.subtract), reads=[ta, tb], writes=[q_r])
                P.op("gpsimd", lambda e: e.tensor_tensor(out=tc[:, :, :], in0=x1, in1=sinb, op=ALU.mult), reads=[q_n, r_p], writes=[tc])
                P.op("vector", lambda e: e.tensor_tensor(out=td[:, :, :], in0=x2, in1=cosb, op=ALU.mult), reads=[q_n, r_p], writes=[td])
                P.op("gpsimd", lambda e: e.tensor_tensor(out=q_r[:, :, 32:64], in0=tc[:, :, :], in1=td[:, :, :], op=ALU.add), reads=[tc, td], awrites=[q_r])
            yield
            pk = psr.get()
            mm(pk[:, 0:128], q_r[:, 8:10, :], self.ident_b[:, :], True, True, [q_r, self.ident_b], pk, True)
            kcol = blk * 128 if is_ctx else NCTX + (blk - 2) * 128
            P.op("vector", lambda e: e.tensor_copy(out=kT[:, kcol:kcol + 128], in_=pk[:, 0:128]), reads=[pk], awrites=[kT])
            if is_ctx:
                return
            yield
            pqt = psr.get()
            for a in range(4):
                mm(pqt[:, a * 128:(a + 1) * 128], q_r[:, 2 * a:2 * a + 2, :], self.ident_b[:, :], True, True, [q_r, self.ident_b], pqt, a == 0)
            qT = qTs.get()
            P.op("scalar", lambda e: e.activation(out=qT[:, :, :], in_=pqt[:, :].rearrange("p (a b) -> p a b", a=4), func=AF.Copy), reads=[pqt], writes=[qT])
            res["qT"] = qT

        def attend(i, qT):
            kbs = [("c", 0, 0), ("c", 1, 1)]
            if i > 0:
                kbs.append(("p", 2 + i - 1, 2 + i - 1))
            kbs.append(("m", 2 + i, 2 + i))
            if i < 31:
                kbs.append(("n", 2 + i + 1, 2 + i + 1))
            O = Ot.get()
            if i == 1 and "qT" in self.taps:
                o_ = self.dout("tap_qT", [128, 4, 128], BF16)
                P.dma("sync", o_.ap(), qT[:, :, :], reads=[qT], writes=[o_], sembuf=o_)
                o_ = self.dout("tap_kT", [128, NCTX + NT], BF16)
                P.dma("sync", o_.ap(), kT[:, :], reads=[kT], writes=[o_], sembuf=o_)
                o_ = self.dout("tap_V", [128, 34, 2, 65], BF16)
                P.dma("sync", o_.ap(), V[:, :, :, :], reads=[V], writes=[o_], sembuf=o_)
            for hk in range(2):
                pT = pTs.get()
                pr = slice(hk * 64, (hk + 1) * 64)
                for bi, (kind, kcb, vb) in enumerate(kbs):
                    pss = psr.get()
                    mm(pss[:, :], kT[pr, kcb * 128:(kcb + 1) * 128], qT[pr, :, :], True, True, [kT, qT], pss, True)
                    if i == 1 and "qT" in self.taps and bi == 0:
                        sc_ = P.sbuf(name + "sc", [128, 512], F32)
                        P.op("vector", lambda e, pss=pss, sc_=sc_: e.tensor_copy(out=sc_[:, :], in_=pss[:, :]), reads=[pss], writes=[sc_])
                        o_ = self.dout("tap_sc%d" % hk, [128, 512], F32)
                        P.dma("sync", o_.ap(), sc_[:, :], reads=[sc_], writes=[o_], sembuf=o_)
                    P.op("scalar", lambda e, pss=pss, bi=bi, pT=pT: e.activation(out=pT[:, bi, :], in_=pss[:, :], func=AF.Exp), reads=[pss],
                         writes=[pT] if bi == 0 else [], awrites=[] if bi == 0 else [pT])
                    if bi % 2 == 1:
                        yield
                    if kind in ("p", "n"):
                        mi = 0 if kind == "p" else 1
                        P.op("gpsimd", lambda e, bi=bi, mi=mi, pT=pT: e.tensor_tensor(
                            out=pT[:, bi, :].rearrange("p (a r) -> p a r", a=4), in0=pT[:, bi, :].rearrange("p (a r) -> p a r", a=4),
                            in1=mk[:, mi, :].unsqueeze(1).to_broadcast([128, 4, 128]), op=ALU.mult), reads=[pT, mk], awrites=[pT])
                if i == 1 and "qT" in self.taps:
                    o_ = self.dout("tap_pT%d" % hk, [128, 5, 512], BF16)
                    P.dma("sync", o_.ap(), pT[:, :, :], reads=[pT], writes=[o_], sembuf=o_)
                po = psr.get()
                for a in range(4):
                    for bi, (kind, kcb, vb) in enumerate(kbs):
                        mm(po[:, a * 128:a * 128 + 65], pT[:, bi, a * 128:(a + 1) * 128], V[:, vb, hk, :], bi == 0, bi == len(kbs) - 1,
                           [pT, V], po, a == 0 and bi == 0)
                yield
                sm_ = sm.get()
                pov = po[:, :].rearrange("p (a c) -> p a c", a=4)
                P.op("vector", lambda e, pov=pov, sm_=sm_, hk=hk: e.tensor_tensor(out=sm_[:, 0:4], in0=pov[:, :, 64], in1=esink[:, hk * 4:hk * 4 + 4], op=ALU.add),
                     reads=[po, esink], writes=[sm_])
                P.op("vector", lambda e, sm_=sm_: e.reciprocal(out=sm_[:, 4:8], in_=sm_[:, 0:4]), reads=[sm_], writes=[sm_])
                P.op("vector", lambda e, pov=pov, sm_=sm_, hk=hk: e.tensor_tensor(out=O[:, hk * 4:hk * 4 + 4, :], in0=pov[:, :, 0:64],
                                                                               in1=sm_[:, 4:8].unsqueeze(2).to_broadcast([128, 4, 64]), op=ALU.mult),
                     reads=[po, sm_], writes=[O] if hk == 0 else [], awrites=[] if hk == 0 else [O])
            yield
            pot = psr.get()
            for c in range(4):
                mm(pot[:, c * 128:(c + 1) * 128], O[:, 2 * c:2 * c + 2, :], self.ident_b[:, :], True, True, [O, self.ident_b], pot, c == 0)
            P.op("scalar", lambda e: e.activation(out=attnT[:, :, i * 128:(i + 1) * 128], in_=pot[:, :].rearrange("p (a b) -> p a b", a=4), func=AF.Copy),
                 reads=[pot], awrites=[attnT])

        def run(*gens):
            gens = [g for g in gens if g is not None]
            while gens:
                for g in list(gens):
                    try:
                        next(g)
                    except StopIteration:
                        gens.remove(g)

        xm0 = pre(0, True)
        xm1 = pre(1, True)
        run(project(0, True, xm0, {}))
        xmn = pre(2, False)
        run(project(1, True, xm1, {}))
        qTs_by_blk = {}
        for j in range(32):
            xm = xmn
            xmn = pre(2 + j + 1, False) if j < 31 else None
            res = {}
            run(project(2 + j, False, xm, res), attend(j - 2, qTs_by_blk.pop(j - 2)) if j >= 2 else None)
            qTs_by_blk[j] = res["qT"]
        run(attend(30, qTs_by_blk.pop(30)))
        run(attend(31, qTs_by_blk.pop(31)))
        P.pop()
        P.pop()


    def tt(self, eng, out, in0, in1, op, reads, writes=(), awrites=()):
        self.P.op(eng, lambda e: e.tensor_tensor(out=out, in0=in0, in1=in1, op=op), reads, writes, awrites)

    def ts(self, eng, out, in0, s1, s2, op0, op1=None, reads=(), writes=(), awrites=()):
        if op1 is None:
            self.P.op(eng, lambda e: e.tensor_scalar(out=out, in0=in0, scalar1=s1, scalar2=None, op0=op0), reads, writes, awrites)
        else:
            self.P.op(eng, lambda e: e.tensor_scalar(out=out, in0=in0, scalar1=s1, scalar2=s2, op0=op0, op1=op1), reads, writes, awrites)

    def stt(self, eng, out, in0, scalar, in1, op0, op1, reads, writes=(), awrites=()):
        self.P.op(eng, lambda e: e.scalar_tensor_tensor(out=out, in0=in0, scalar=scalar, in1=in1, op0=op0, op1=op1), reads, writes, awrites)

    def act(self, out, in_, func, reads, writes=(), awrites=(), **kw):
        self.P.op("scalar", lambda e: e.activation(out=out, in_=in_, func=func, **kw), reads, writes, awrites)

    def cp(self, eng, out, in_, reads, writes=(), awrites=()):
        self.P.op(eng, lambda e: e.tensor_copy(out=out, in_=in_), reads, writes, awrites)

    def mmul(self, ps_ap, lhsT, rhs, first, last, reads, psb, first_in_bank):
        self.P.op("tensor", lambda e: e.matmul(ps_ap, lhsT=lhsT, rhs=rhs, start=first, stop=last),
                  reads=reads, writes=[psb] if first_in_bank else [], awrites=[] if first_in_bank else [psb])

    def sin_reduced(self, out_b, out_ap, phi_b, phi_ap, add, kk, r, shape_sl, eng="vector"):
        k_ap, r_ap = shape_sl(kk), shape_sl(r)
        if add != 0.0:
            self.ts(eng, r_ap, phi_ap, add, None, ALU.add, reads=[phi_b], writes=[r])
            src_b, src_ap = r, r_ap
        else:
            src_b, src_ap = phi_b, phi_ap
        self.ts(eng, k_ap, src_ap, 1.0 / TWO_PI, MAGIC, ALU.mult, ALU.add, reads=[src_b], writes=[kk])
        self.ts(eng, k_ap, k_ap, -MAGIC, None, ALU.add, reads=[kk], writes=[kk])
        if eng == "vector":
            self.stt(eng, r_ap, k_ap, -CW1, src_ap, ALU.mult, ALU.add, reads=[kk, src_b], writes=[r])
            self.stt(eng, r_ap, k_ap, -CW2, r_ap, ALU.mult, ALU.add, reads=[kk, r], writes=[r])
        else:
            assert add != 0.0
            self.ts(eng, shape_sl(self._sr_tmp), k_ap, -CW1, None, ALU.mult, reads=[kk], writes=[self._sr_tmp])
            self.tt(eng, r_ap, r_ap, shape_sl(self._sr_tmp), ALU.add, [r, self._sr_tmp], [r])
            self.ts(eng, k_ap, k_ap, -CW2, None, ALU.mult, reads=[kk], writes=[kk])
            self.tt(eng, r_ap, r_ap, k_ap, ALU.add, [r, kk], [r])
        self.ts(eng, r_ap, r_ap, -math.pi, math.pi, ALU.max, ALU.min, reads=[r], writes=[r])
        self.act(out_ap, r_ap, AF.Sin, reads=[r], writes=[out_b])

    def mix_ssm(self, name, uT):
        P = self.P
        P.push()
        psr = Rot(self.ps)
        R = P.sbuf(name + "R", [64, 128], F32)
        for a, arr in enumerate((self.lam_re, self.lam_im)):
            for d in range(2):
                P.dma("sync", R[(a * 2 + d) * 16:(a * 2 + d + 1) * 16, :], arr.ap()[d].rearrange("(k gl) p -> k (gl p)", gl=2),
                      reads=[arr], awrites=[R], sembuf=R)
        pl = psr.get()
        self.mmul(pl[:, 0:64], R[:, :], self.ident_f[0:64, 0:64], True, True, [R, self.ident_f], pl, True)
        LL = P.sbuf(name + "LL", [128, 64], F32)
        self.cp("vector", LL[:, :], pl[:, 0:64], [pl], [LL])
        T2 = P.sbuf(name + "T2", [32, 2], F32)
        P.dma("sync", T2[:, :], self.log_dt.ap().rearrange("d (k gl) -> (d k) gl", gl=2), reads=[self.log_dt], writes=[T2], sembuf=T2)
        E2 = P.sbuf(name + "E2", [32, 128], F32)
        self.cp("vector", E2[:, :].rearrange("r (g p) -> r g p", g=2), T2[:, :].unsqueeze(2).to_broadcast([32, 2, 64]), [T2], [E2])
        pd = psr.get()
        self.mmul(pd[:, 0:32], E2[:, :], self.ident_f[0:32, 0:32], True, True, [E2, self.ident_f], pd, True)
        sc = {}
        for nm in ("dt", "t", "rho", "th", "sn", "cs", "ar", "ai", "nr", "den", "fr", "fi", "u1", "u2", "kk", "rr", "th2", "rho2"):
            sc[nm] = P.sbuf(name + "p_" + nm, [128, 32], F32)
        full = lambda b: b[:, :]
        lr, li = LL[:, 0:32], LL[:, 32:64]
        self.act(sc["dt"][:, :], pd[:, 0:32], AF.Exp, [pd], [sc["dt"]])
        self.tt("vector", sc["t"][:, :], lr, sc["dt"][:, :], ALU.mult, [LL, sc["dt"]], [sc["t"]])
        self.act(sc["rho"][:, :], sc["t"][:, :], AF.Exp, [sc["t"]], [sc["rho"]])
        self.tt("vector", sc["th"][:, :], li, sc["dt"][:, :], ALU.mult, [LL, sc["dt"]], [sc["th"]])
        self.sin_reduced(sc["sn"], sc["sn"][:, :], sc["th"], sc["th"][:, :], 0.0, sc["kk"], sc["rr"], full)
        self.sin_reduced(sc["cs"], sc["cs"][:, :], sc["th"], sc["th"][:, :], math.pi / 2, sc["kk"], sc["rr"], full)
        self.tt("vector", sc["ar"][:, :], sc["rho"][:, :], sc["cs"][:, :], ALU.mult, [sc["rho"], sc["cs"]], [sc["ar"]])
        self.tt("vector", sc["ai"][:, :], sc["rho"][:, :], sc["sn"][:, :], ALU.mult, [sc["rho"], sc["sn"]], [sc["ai"]])
        self.ts("vector", sc["nr"][:, :], sc["ar"][:, :], -1.0, None, ALU.add, reads=[sc["ar"]], writes=[sc["nr"]])
        self.tt("vector", sc["u1"][:, :], lr, lr, ALU.mult, [LL], [sc["u1"]])
        self.tt("vector", sc["u2"][:, :], li, li, ALU.mult, [LL], [sc["u2"]])
        self.tt("vector", sc["den"][:, :], sc["u1"][:, :], sc["u2"][:, :], ALU.add, [sc["u1"], sc["u2"]], [sc["den"]])
        P.op("vector", lambda e: e.reciprocal(out=sc["den"][:, :], in_=sc["den"][:, :]), reads=[sc["den"]], writes=[sc["den"]])
        self.tt("vector", sc["u1"][:, :], sc["nr"][:, :], lr, ALU.mult, [sc["nr"], LL], [sc["u1"]])
        self.tt("vector", sc["u2"][:, :], sc["ai"][:, :], li, ALU.mult, [sc["ai"], LL], [sc["u2"]])
        self.tt("vector", sc["fr"][:, :], sc["u1"][:, :], sc["u2"][:, :], ALU.add, [sc["u1"], sc["u2"]], [sc["fr"]])
        self.tt("vector", sc["fr"][:, :], sc["fr"][:, :], sc["den"][:, :], ALU.mult, [sc["fr"], sc["den"]], [sc["fr"]])
        self.tt("vector", sc["u1"][:, :], sc["ai"][:, :], lr, ALU.mult, [sc["ai"], LL], [sc["u1"]])
        self.tt("vector", sc["u2"][:, :], sc["nr"][:, :], li, ALU.mult, [sc["nr"], LL], [sc["u2"]])
        self.tt("vector", sc["fi"][:, :], sc["u1"][:, :], sc["u2"][:, :], ALU.subtract, [sc["u1"], sc["u2"]], [sc["fi"]])
        self.tt("vector", sc["fi"][:, :], sc["fi"][:, :], sc["den"][:, :], ALU.mult, [sc["fi"], sc["den"]], [sc["fi"]])
        LD = 4
        pw = {1: (sc["ar"], sc["ai"])}
        for m in range(2, LD + 1):
            pr = P.sbuf(name + "apr%d" % m, [128, 32], F32)
            pi_ = P.sbuf(name + "api%d" % m, [128, 32], F32)
            qr, qi = pw[m - 1]
            self.tt("vector", pr[:, :], qr[:, :], sc["ar"][:, :], ALU.mult, [qr, sc["ar"]], [pr])
            self.tt("vector", sc["u1"][:, :], qi[:, :], sc["ai"][:, :], ALU.mult, [qi, sc["ai"]], [sc["u1"]])
            self.tt("vector", pr[:, :], pr[:, :], sc["u1"][:, :], ALU.subtract, [pr, sc["u1"]], [pr])
            self.tt("vector", pi_[:, :], qr[:, :], sc["ai"][:, :], ALU.mult, [qr, sc["ai"]], [pi_])
            self.tt("vector", sc["u1"][:, :], qi[:, :], sc["ar"][:, :], ALU.mult, [qi, sc["ar"]], [sc["u1"]])
            self.tt("vector", pi_[:, :], pi_[:, :], sc["u1"][:, :], ALU.add, [pi_, sc["u1"]], [pi_])
            pw[m] = (pr, pi_)
        self.ts("vector", sc["th2"][:, :], sc["th"][:, :], float(LD), None, ALU.mult, reads=[sc["th"]], writes=[sc["th2"]])
        self.tt("vector", sc["rho2"][:, :], sc["rho"][:, :], sc["rho"][:, :], ALU.mult, [sc["rho"]], [sc["rho2"]])
        for _ in range(LD - 2):
            self.tt("vector", sc["rho2"][:, :], sc["rho2"][:, :], sc["rho"][:, :], ALU.mult, [sc["rho2"], sc["rho"]], [sc["rho2"]])
        Bmat = P.sbuf(name + "Bmat", [128, 16, LD, 2, 128], BF16)
        Cmat = P.sbuf(name + "Cmat", [128, 16, LD, 2, 128], BF16)
        Ks = P.sbuf(name + "Ks", [128, LD - 1, 4, 128], BF16)

        def prep_dir(d):
            P.push()
            dsl = slice(d * 16, (d + 1) * 16)
            CR2 = Rot([P.sbuf(name + "CR2%d" % i, [128, 4, 128], F32) for i in range(2)])
            P.op("vector", lambda e: e.memset(Cmat[:, :, :, :, :], 0.0), writes=[Cmat])
            for ri, arr in enumerate((self.c_re, self.c_im)):
                C2 = CR2.get()
                for dup in range(2):
                    P.dma("sync", C2[:, :, dup * 64:(dup + 1) * 64], arr.ap()[d].rearrange("(c g8) j p -> (g8 j) c p", c=4), reads=[arr],
                          writes=[C2] if dup == 0 else [], awrites=[] if dup == 0 else [C2], sembuf=C2)
                pc = psr.get()
                for cq in range(4):
                    self.mmul(pc[:, cq * 128:(cq + 1) * 128], C2[:, cq, :], self.ident_f[:, :], True, True, [C2, self.ident_f], pc, cq == 0)
                for gl in range(2):
                    for kl in range(4):
                        rows = slice(gl * 64, (gl + 1) * 64)
                        c0 = 32 * kl + 16 * gl
                        dst = Cmat[rows, :, 0, ri, :].rearrange("q (c k) x -> q c k x", k=4)[:, :, kl, c0:c0 + 16]
                        srcv = pc[rows, :].rearrange("q (c x) -> q c x", c=4)[:, :, c0:c0 + 16]
                        if ri == 0:
                            self.cp("vector", dst, srcv, [pc], awrites=[Cmat])
                        else:
                            self.ts("vector", dst, srcv, -1.0, None, ALU.mult, reads=[pc], awrites=[Cmat])
            ctmp_ = P.sbuf(name + "ctmpv", [128, 128], F32)
            for v in range(1, LD):
                pr, pi_ = pw[v]
                for k in range(16):
                    dk = d * 16 + k
                    X0, X1 = Cmat[:, k, 0, 0, :], Cmat[:, k, 0, 1, :]
                    ar_, ai_ = pr[:, dk:dk + 1], pi_[:, dk:dk + 1]
                    self.ts("vector", ctmp_[:, :], X1, ai_, None, ALU.mult, reads=[Cmat, pi_], writes=[ctmp_])
                    self.stt("vector", Cmat[:, k, v, 0, :], X0, ar_, ctmp_[:, :], ALU.mult, ALU.add, reads=[Cmat, pr, ctmp_], awrites=[Cmat])
                    self.ts("vector", ctmp_[:, :], X0, ai_, None, ALU.mult, reads=[Cmat, pi_], writes=[ctmp_])
                    self.stt("vector", Cmat[:, k, v, 1, :], X1, ar_, ctmp_[:, :], ALU.mult, ALU.subtract, reads=[Cmat, pr, ctmp_], awrites=[Cmat])
            Bre = P.sbuf(name + "Bre", [128, 16, 16], F32)
            Bim = P.sbuf(name + "Bim", [128, 16, 16], F32)
            P.dma("sync", Bre[:, :, :], self.b_re.ap()[d].rearrange("(k gl) p i -> (gl p) k i", gl=2), reads=[self.b_re], writes=[Bre], sembuf=Bre)
            P.dma("sync", Bim[:, :, :], self.b_im.ap()[d].rearrange("(k gl) p i -> (gl p) k i", gl=2), reads=[self.b_im], writes=[Bim], sembuf=Bim)
            bbr = P.sbuf(name + "bbr", [128, 16, 16], F32)
            bbi = P.sbuf(name + "bbi", [128, 16, 16], F32)
            bmr = P.sbuf(name + "bmr", [128, 16, 16], F32)
            bmi = P.sbuf(name + "bmi", [128, 16, 16], F32)
            tmp = P.sbuf(name + "btmp", [128, 16, 16], F32)

            def cscale(outr, outi, inr, ini, sr, si):
                srb = sr[:, dsl].unsqueeze(2).to_broadcast([128, 16, 16])
                sib = si[:, dsl].unsqueeze(2).to_broadcast([128, 16, 16])
                self.tt("vector", outr[:, :, :], inr[:, :, :], srb, ALU.mult, [inr, sr], [outr])
                self.tt("vector", tmp[:, :, :], ini[:, :, :], sib, ALU.mult, [ini, si], [tmp])
                self.tt("vector", outr[:, :, :], outr[:, :, :], tmp[:, :, :], ALU.subtract, [outr, tmp], [outr])
                self.tt("vector", outi[:, :, :], ini[:, :, :], srb, ALU.mult, [ini, sr], [outi])
                self.tt("vector", tmp[:, :, :], inr[:, :, :], sib, ALU.mult, [inr, si], [tmp])
                self.tt("vector", outi[:, :, :], outi[:, :, :], tmp[:, :, :], ALU.add, [outi, tmp], [outi])

            cscale(bbr, bbi, Bre, Bim, sc["fr"], sc["fi"])
            Eall = Rot([P.sbuf(name + "Eall%d" % i, [128, 16, 128], BF16) for i in range(2)])
            for m_ in range(LD):
                if m_ == 0:
                    srcs = (bbr, bbi)
                else:
                    cscale(bmr, bmi, bbr, bbi, pw[m_][0], pw[m_][1])
                    srcs = (bmr, bmi)
                Ekeep = {}
                for ri in range(2):
                    bb = srcs[ri]
                    E = Eall.get()
                    Ekeep[ri] = E
                    self.P.op("vector", lambda e, E=E: e.memset(E[:, :, :], 0.0), writes=[E])
                    for gl in range(2):
                        for kl in range(4):
                            rows = slice(gl * 64, (gl + 1) * 64)
                            c0 = 32 * kl + 16 * gl
                            dst = E[rows, :, :].rearrange("q (c k) x -> q c k x", k=4)[:, :, kl, c0:c0 + 16]
                            srcv = bb[rows, :, :].rearrange("q (c k) i -> q c k i", k=4)[:, :, kl, :]
                            self.cp("vector", dst, srcv, [bb], awrites=[E])
                    for k4 in range(4):
                        pb = psr.get()
                        for j in range(4):
                            self.mmul(pb[:, j * 128:(j + 1) * 128], E[:, k4 * 4 + j, :], self.ident_b[:, :], True, True, [E, self.ident_b], pb, j == 0)
                        self.act(Bmat[:, k4 * 4:(k4 + 1) * 4, m_, ri, :], pb[:, :].rearrange("p (a b) -> p a b", a=4), AF.Copy, [pb], awrites=[Bmat])
                if m_ <= LD - 2:
                    pk0 = psr.get()
                    for cq in range(4):
                        i_ = 0
                        for kl in range(4):
                            for r2 in range(2):
                                self.mmul(pk0[:, cq * 128:(cq + 1) * 128], Ekeep[r2][:, cq * 4 + kl, :], Cmat[:, cq * 4 + kl, 0, r2, :],
                                          i_ == 0, i_ == 7, [Ekeep[r2], Cmat], pk0, cq == 0 and i_ == 0)
                                i_ += 1
                    self.act(Ks[:, m_, :, :], pk0[:, :].rearrange("p (a b) -> p a b", a=4), AF.Copy, [pk0], awrites=[Ks])
            P.pop()

        dcol = P.sbuf(name + "dcol", [128, 4], F32)
        for cq in range(4):
            P.dma("sync", dcol[:, cq:cq + 1], self.ssm_d.ap()[0, cq * 128:(cq + 1) * 128].rearrange("(p o) -> p o", o=1), reads=[self.ssm_d], awrites=[dcol], sembuf=dcol)
        Dg = P.sbuf(name + "Dg", [128, 4, 128], BF16)
        for cq in range(4):
            self.ts("vector", Dg[:, cq, :], self.ident_f[:, :], dcol[:, cq:cq + 1], None, ALU.mult, reads=[self.ident_f, dcol], awrites=[Dg])
        wg = P.sbuf(name + "wg", [128, 4, 512], BF16)
        P.dma("gpsimd", wg[:, :, :], self.w_glu.ap().rearrange("(k p) n -> p k n", p=128), reads=[self.w_glu], writes=[wg], sembuf=wg)
        IO = P.sbuf(name + "IO", [128, 512], F32)
        P.dma("sync", IO[:, :], self.c_iota.ap()[0:1, :].partition_broadcast(128), reads=[self.c_iota], writes=[IO], sembuf=IO)
        yf = P.dram(name + "yf", [4, 128, NT], F32)
        N2 = 512 // LD
        cs_t = [P.sbuf(name + "cs%d" % i, [128, N2], F32) for i in range(4)]
        sn_t = [P.sbuf(name + "sn%d" % i, [128, N2], F32) for i in range(4)]
        rho_t = [P.sbuf(name + "rho%d" % i, [128, N2], F32) for i in range(4)]
        hst = [P.sbuf(name + "hst%d" % i, [128, 2], F32) for i in range(4)]
        phi = P.sbuf(name + "phi", [128, N2], F32)
        kk = P.sbuf(name + "kk", [128, N2], F32)
        rr = P.sbuf(name + "rr", [128, N2], F32)
        WA = [P.sbuf(name + "wA%d" % i, [128, N2], F32) for i in range(4)]
        WB = [P.sbuf(name + "wB%d" % i, [128, N2], F32) for i in range(4)]
        WC = [P.sbuf(name + "wC%d" % i, [128, N2], F32) for i in range(4)]
        WD = [P.sbuf(name + "wD%d" % i, [128, N2], F32) for i in range(4)]
        HR = [P.sbuf(name + "hbr%d" % i, [128, N2 + 1], BF16) for i in range(4)]
        HI = [P.sbuf(name + "hbi%d" % i, [128, N2 + 1], BF16) for i in range(4)]
        yfs = Rot([P.sbuf(name + "yfs%d" % i, [128, 512], F32) for i in range(2)])
        yts = Rot([P.sbuf(name + "yts%d" % i, [128, 512], F32) for i in range(2)])
        gl1 = Rot([P.sbuf(name + "gl1%d" % i, [128, 512], F32) for i in range(2)])
        gl2 = Rot([P.sbuf(name + "gl2%d" % i, [128, 512], F32) for i in range(2)])
        slN2 = lambda b_: b_[:, :]
        psx = Rot(self.ps[0:4])
        acc = self.ps[4:4 + LD]

        def scan_op(out_b, rho, in_b, h_, col, n):
            P.op("vector", lambda e: e.tensor_tensor_scan(out=out_b[:, 0:n], data0=rho[:, 0:n], data1=in_b[:, 0:n],
                                                          initial=h_[:, col:col + 1], op0=ALU.mult, op1=ALU.add),
                 reads=[rho, in_b, h_], writes=[out_b])

        for d in range(2):
            prep_dir(d)
            for c in range(4):
                for kl in range(4):
                    dk = d * 16 + c * 4 + kl
                    self.ts("vector", phi[:, :], IO[:, 0:N2], sc["th2"][:, dk:dk + 1], None, ALU.mult, reads=[IO, sc["th2"]], writes=[phi])
                    self.sin_reduced(sn_t[kl], sn_t[kl][:, :], phi, phi[:, :], 0.0, kk, rr, slN2)
                    self.sin_reduced(cs_t[kl], cs_t[kl][:, :], phi, phi[:, :], math.pi / 2, kk, rr, slN2)
                    self.ts("gpsimd", rho_t[kl][:, :], IO[:, 0:N2], 0.0, sc["rho2"][:, dk:dk + 1], ALU.mult, ALU.add, reads=[IO, sc["rho2"]], writes=[rho_t[kl]])
                    P.op("vector", lambda e, h=hst[kl]: e.memset(h[:, :], 0.0), writes=[hst[kl]])
                order = [0] + (list(range(1, 9)) if d == 0 else list(range(8, 0, -1)))
                for n_i in order:
                    ut = uT[n_i]
                    n = 256 if n_i == 0 else 512
                    n2 = n // LD
                    ub_ = ut[:, c, :]
                    if d == 0:
                        us = [sub_last(ub_, sg_, LD, n2) for sg_ in range(LD)]
                    else:
                        us = [sub_last(ub_, n - 1 - sg_, -LD, n2) for sg_ in range(LD)]
                    readout = n_i > 0
                    for KL in ((0, 1), (2, 3)):
                        px = {}
                        for kl in KL:
                            k = c * 4 + kl
                            pxr, pxi = psx.get(), psx.get()
                            px[kl] = (pxr, pxi)
                            for ri, pacc in ((0, pxr), (1, pxi)):
                                for sg_ in range(LD):
                                    self.mmul(pacc[:, 0:n2], Bmat[:, k, LD - 1 - sg_, ri, :], us[sg_], sg_ == 0, sg_ == LD - 1, [Bmat, ut], pacc, sg_ == 0)
                        for kl in KL:
                            if readout:
                                self.act(HR[kl][:, 0:1], hst[kl][:, 0:1], AF.Copy, [hst[kl]], writes=[HR[kl]])
                                self.act(HI[kl][:, 0:1], hst[kl][:, 1:2], AF.Copy, [hst[kl]], writes=[HI[kl]])
                        for kl in KL:
                            pxr, pxi = px[kl]
                            cs, sn = cs_t[kl], sn_t[kl]
                            self.tt("vector", WA[kl][:, 0:n2], pxr[:, 0:n2], cs[:, 0:n2], ALU.mult, [pxr, cs], [WA[kl]])
                            self.tt("vector", WB[kl][:, 0:n2], pxi[:, 0:n2], sn[:, 0:n2], ALU.mult, [pxi, sn], [WB[kl]])
                            self.tt("vector", WC[kl][:, 0:n2], pxi[:, 0:n2], cs[:, 0:n2], ALU.mult, [pxi, cs], [WC[kl]])
                            self.tt("vector", WD[kl][:, 0:n2], pxr[:, 0:n2], sn[:, 0:n2], ALU.mult, [pxr, sn], [WD[kl]])
                        for kl in KL:
                            self.tt("gpsimd", WA[kl][:, 0:n2], WA[kl][:, 0:n2], WB[kl][:, 0:n2], ALU.add, [WA[kl], WB[kl]], [WA[kl]])
                            self.tt("gpsimd", WC[kl][:, 0:n2], WC[kl][:, 0:n2], WD[kl][:, 0:n2], ALU.subtract, [WC[kl], WD[kl]], [WC[kl]])
                        for kl in KL:
                            scan_op(WB[kl], rho_t[kl], WA[kl], hst[kl], 0, n2)
                            scan_op(WD[kl], rho_t[kl], WC[kl], hst[kl], 1, n2)
                        for kl in KL:
                            cs, sn = cs_t[kl], sn_t[kl]
                            self.tt("vector", WA[kl][:, 0:n2], WB[kl][:, 0:n2], cs[:, 0:n2], ALU.mult, [WB[kl], cs], [WA[kl]])
                            self.tt("gpsimd", WC[kl][:, 0:n2], WD[kl][:, 0:n2], sn[:, 0:n2], ALU.mult, [WD[kl], sn], [WC[kl]])
                            self.tt("vector", WB[kl][:, 0:n2], WB[kl][:, 0:n2], sn[:, 0:n2], ALU.mult, [WB[kl], sn], [WB[kl]])
                            self.tt("gpsimd", WD[kl][:, 0:n2], WD[kl][:, 0:n2], cs[:, 0:n2], ALU.mult, [WD[kl], cs], [WD[kl]])
                        for kl in KL:
                            self.tt("vector", WA[kl][:, 0:n2], WA[kl][:, 0:n2], WC[kl][:, 0:n2], ALU.subtract, [WA[kl], WC[kl]], [WA[kl]])
                            self.tt("vector", WB[kl][:, 0:n2], WB[kl][:, 0:n2], WD[kl][:, 0:n2], ALU.add, [WB[kl], WD[kl]], [WB[kl]])
                        for kl in KL:
                            h_ = hst[kl]
                            self.act(h_[:, 0:1], WA[kl][:, n2 - 1:n2], AF.Copy, [WA[kl]], writes=[h_])
                            self.act(h_[:, 1:2], WB[kl][:, n2 - 1:n2], AF.Copy, [WB[kl]], awrites=[h_])
                            if readout:
                                self.act(HR[kl][:, 1:n2 + 1], WA[kl][:, 0:n2], AF.Copy, [WA[kl]], awrites=[HR[kl]])
                                self.act(HI[kl][:, 1:n2 + 1], WB[kl][:, 0:n2], AF.Copy, [WB[kl]], awrites=[HI[kl]])
                    if not readout:
                        continue
                    for tau in range(LD):
                        pa = acc[tau]
                        nmm = 8 + ((tau + 1) if tau < LD - 1 else 0) + (1 if d == 1 else 0)
                        i_ = 0
                        for kl in range(4):
                            k = c * 4 + kl
                            for ri, Hs in ((0, HR[kl]), (1, HI[kl])):
                                if tau == LD - 1:
                                    self.mmul(pa[:, 0:n2], Cmat[:, k, 0, ri, :], Hs[:, 1:n2 + 1], i_ == 0, i_ == nmm - 1, [Cmat, Hs], pa, i_ == 0)
                                else:
                                    self.mmul(pa[:, 0:n2], Cmat[:, k, tau + 1, ri, :], Hs[:, 0:n2], i_ == 0, i_ == nmm - 1, [Cmat, Hs], pa, i_ == 0)
                                i_ += 1
                        if tau < LD - 1:
                            for sg_ in range(tau + 1):
                                self.mmul(pa[:, 0:n2], Ks[:, tau - sg_, c, :], us[sg_], False, i_ == nmm - 1, [Ks, ut], pa, False)
                                i_ += 1
                        if d == 1:
                            self.mmul(pa[:, 0:n2], Dg[:, c, :], us[tau], False, True, [Dg, ut], pa, False)
                    t0 = (n_i - 1) * 512
                    if d == 0:
                        yt_ = yts.get()
                        for tau in range(LD):
                            self.act(sub_last(yt_[:, :], tau, LD, n2), acc[tau][:, 0:n2], AF.Copy, [acc[tau]],
                                     writes=[yt_] if tau == 0 else [], awrites=[] if tau == 0 else [yt_])
                        P.dma("scalar", yf.ap()[c][:, t0:t0 + 512], yt_[:, :], reads=[yt_], awrites=[yf], sembuf=yt_)
                    else:
                        yf_ = yfs.get()
                        P.dma("sync", yf_[:, :], yf.ap()[c][:, t0:t0 + 512], reads=[yf], writes=[yf_], sembuf=yf_)
                        yt_ = yts.get()
                        for tau in range(LD):
                            self.tt("vector", sub_last(yt_[:, :], n - 1 - tau, -LD, n2), acc[tau][:, 0:n2], sub_last(yf_[:, :], n - 1 - tau, -LD, n2), ALU.add,
                                    [acc[tau], yf_], writes=[yt_] if tau == 0 else [], awrites=[] if tau == 0 else [yt_])
                        g1, g2 = gl1.get(), gl2.get()
                        self.tt("gpsimd", g1[:, :], yt_[:, :], yt_[:, :], ALU.mult, [yt_], [g1])
                        self.ts("gpsimd", g1[:, :], g1[:, :], 0.044715, 1.0, ALU.mult, ALU.add, reads=[g1], writes=[g1])
                        self.tt("gpsimd", g1[:, :], g1[:, :], yt_[:, :], ALU.mult, [g1, yt_], [g1])
                        self.act(g2[:, :], g1[:, :], AF.Sigmoid, [g1], [g2], scale=2.0 * math.sqrt(2.0 / math.pi))
                        self.tt("vector", ut[:, c, :], yt_[:, :], g2[:, :], ALU.mult, [yt_, g2], awrites=[ut])
        if "yf" in self.taps:
            o_ = self.dout("tap_yf", [4, 128, NT], F32)
            P.dma("sync", o_.ap(), yf.ap(), reads=[yf], writes=[o_], sembuf=o_)
        sg = Rot([P.sbuf(name + "sg%d" % i, [128, 4, 512], BF16) for i in range(2)])
        for n_i in range(1, 9):
            ut = uT[n_i]
            s_ = sg.get()
            for m in range(4):
                pz = psr.get()
                for kc in range(4):
                    self.mmul(pz[:, :], wg[:, kc, m * 128:(m + 1) * 128], ut[:, kc, :], kc == 0, kc == 3, [wg, ut], pz, kc == 0)
                self.act(s_[:, m, :], pz[:, :], AF.Sigmoid, [pz], writes=[s_] if m == 0 else [], awrites=[] if m == 0 else [s_])
            self.tt("vector", ut[:, :, :], ut[:, :, :], s_[:, :, :], ALU.mult, [ut, s_], [ut])
        P.pop()

    def mix_out(self, name, src, dst, attnT, uT):
        P = self.P
        l, site = 0, 1
        P.push()
        wo = P.sbuf(name + "wo", [128, 8, D], BF16)
        P.dma("gpsimd", wo[:, :, :], self.w_out.ap().rearrange("(k p) n -> p k n", p=128), reads=[self.w_out], writes=[wo], sembuf=wo)
        G = P.sbuf(name + "G", [128, D], F32)
        P.dma("sync", G[:, :], self.mrow.ap()[l, 0][(3 * site + 2) * D:(3 * site + 3) * D].partition_broadcast(128), reads=[self.mrow], writes=[G], sembuf=G)
        xr = Rot([P.sbuf(name + "oxr%d" % i, [128, D], F32) for i in range(2)])
        ot = Rot([P.sbuf(name + "oot%d" % i, [128, D], F32) for i in range(2)])
        yt = Rot([P.sbuf(name + "oyt%d" % i, [128, 512], F32) for i in range(2)])
        psr = Rot(self.ps)
        for j in range(32):
            x_r = xr.get()
            P.dma("sync", x_r[:, :], src.ap()[j * 128:(j + 1) * 128, :], reads=[src], writes=[x_r], sembuf=x_r)
            o = ot.get()
            ut = uT[1 + j // 4]
            uc0 = (j % 4) * 128
            for dh in range(2):
                po = psr.get()
                for kc in range(8):
                    if kc < 4:
                        lhsT, rd = attnT[:, kc, j * 128:(j + 1) * 128], attnT
                    else:
                        lhsT, rd = ut[:, kc - 4, uc0:uc0 + 128], ut
                    if kc >= 4 and self.skip_ssm:
                        continue
                    last = (kc == 3) if self.skip_ssm else (kc == 7)
                    P.op("tensor", lambda e, po=po, lhsT=lhsT, kc=kc, dh=dh, last=last: e.matmul(po[:, :], lhsT=lhsT, rhs=wo[:, kc, dh * 512:(dh + 1) * 512],
                                                                                           start=(kc == 0), stop=last),
                         reads=[rd, wo], writes=[po] if kc == 0 else [], awrites=[] if kc == 0 else [po])
                t = yt.get()
                P.op("vector", lambda e, t=t, po=po, dh=dh: e.tensor_tensor(out=t[:, :], in0=po[:, :], in1=G[:, dh * 512:(dh + 1) * 512], op=ALU.mult),
                     reads=[po, G], writes=[t])
                P.op("gpsimd", lambda e, t=t, o=o, x_r=x_r, dh=dh: e.tensor_tensor(out=o[:, dh * 512:(dh + 1) * 512], in0=t[:, :],
                                                                               in1=x_r[:, dh * 512:(dh + 1) * 512], op=ALU.add),
                     reads=[t, x_r], writes=[o] if dh == 0 else [], awrites=[] if dh == 0 else [o])
            P.dma(STQ, dst.ap()[j * 128:(j + 1) * 128, :], o[:, :], reads=[o], awrites=[dst], sembuf=o)
        P.pop()

    def fourier_phase(self, name, l, site, src, dst):
        P = self.P
        if "c_dftn" not in self.inputs:
            self.c_dftn = self.din("c_dftn", [2, 32, 128, 32, 128], BF16)
        for half in range(2):
            self.fourier_half(name, l, site, src, dst, half)

    def fourier_half(self, name, l, site, src, dst, half):
        P = self.P
        if True:
            P.push()
            S, Sh, G = self.load_rows(name + "r", l, 0, site, 1.0)
            s1b = self.make_s1_bufs(name, small=True)
            dc = P.sbuf(name + "dc", [128, 2, 2, 256], BF16)
            for cs in range(2):
                P.dma("gpsimd", dc[:, cs, :, :], self.c_dftc.ap()[cs].rearrange("(k p) j -> p k j", p=128), reads=[self.c_dftc],
                      awrites=[dc], sembuf=dc)
            wf = P.sbuf(name + "wf", [128, 4, D], BF16)
            P.dma("gpsimd", wf[:, :, :], self.w_f.ap()[half * 512:(half + 1) * 512, :].rearrange("(k p) n -> p k n", p=128),
                  reads=[self.w_f], writes=[wf], sembuf=wf)
            AB = P.sbuf(name + "AB", [128, 32, 2, 512], BF16)
            hT = Rot([P.sbuf(name + "hT%d" % i, [128, 8, 128], BF16) for i in range(2)])
            pstr = Rot(self.ps[0:2])
            psab = Rot(self.ps[2:4])
            xm_next = self.stage1a(s1b, src, src.ap()[0:128, :], S, Sh)
            for nb in range(32):
                h = hT.get()
                xm_cur = xm_next
                if nb < 31:
                    xm_next = self.stage1a(s1b, src, src.ap()[(nb + 1) * 128:(nb + 2) * 128, :], S, Sh)
                self.stage1b(xm_cur, h, 0, pstr)
                for cs in range(2):
                    pa = psab.get()
                    for gq in range(2):
                        for kc2 in range(2):
                            kc = (half * 2 + gq) * 2 + kc2
                            P.op("tensor", lambda e, pa=pa, h=h, kc=kc, cs=cs, kc2=kc2, gq=gq: e.matmul(
                                pa[:, gq * 256:(gq + 1) * 256], lhsT=h[:, kc, :], rhs=dc[:, cs, kc2, :], start=(kc2 == 0), stop=(kc2 == 1)),
                                reads=[h, dc], writes=[pa] if (gq == 0 and kc2 == 0) else [], awrites=[] if (gq == 0 and kc2 == 0) else [pa])
                    if cs == 0:
                        P.op("scalar", lambda e, pa=pa, nb=nb: e.activation(out=AB[:, nb, 0, :], in_=pa[:, :], func=AF.Copy), reads=[pa], awrites=[AB])
                    else:
                        P.op("vector", lambda e, pa=pa, nb=nb: e.tensor_copy(out=AB[:, nb, 1, :], in_=pa[:, :]), reads=[pa], awrites=[AB])
            ct = Rot([P.sbuf(name + "ct%d" % i, [128, 32, 128], BF16) for i in range(3)])
            stt = Rot([P.sbuf(name + "sn%d" % i, [128, 32, 128], BF16) for i in range(3)])
            yb = Rot([P.sbuf(name + "yb%d" % i, [128, 512], BF16) for i in range(2)])
            yT = Rot([P.sbuf(name + "yT%d" % i, [128, 4, 128], BF16) for i in range(2)])
            xr = Rot([P.sbuf(name + "xr%d" % i, [128, D], F32) for i in range(2)])
            ot = Rot([P.sbuf(name + "ot%d" % i, [128, D], F32) for i in range(2)])
            yt = Rot([P.sbuf(name + "yt%d" % i, [128, 512], F32) for i in range(2)])
            psy = Rot(self.ps[4:6])
            pso = Rot(self.ps[6:8])
            rsrc = src if half == 0 else dst
            psq = Rot(self.ps[2:6])
            pf = Rot([P.sbuf(name + "pf%d" % i, [128, 512], F32) for i in range(2)])
            qf = Rot([P.sbuf(name + "qf%d" % i, [128, 512], F32) for i in range(2)])
            yd = Rot([P.sbuf(name + "yd%d" % i, [128, 512], BF16) for i in range(2)])
            identJ = rev_last(self.ident_b[:, :])

            def head(kb):
                c = ct.get()
                sn = stt.get()
                P.dma("sync", c[:, :, :], self.c_dftn.ap()[0, kb], reads=[self.c_dftn], writes=[c], sembuf=c)
                P.dma("sync", sn[:, :, :], self.c_dftn.ap()[1, kb], reads=[self.c_dftn], writes=[sn], sembuf=sn)
                pp, pq = psq.get(), psq.get()
                for (tt, ab, pacc) in ((c, 0, pp), (sn, 1, pq)):
                    for nb in range(32):
                        self.mmul(pacc[:, :], tt[:, nb, :], AB[:, nb, ab, :], nb == 0, nb == 31, [tt, AB], pacc, nb == 0)
                p_, q_ = pf.get(), qf.get()
                self.act(p_[:, :], pp[:, :], AF.Copy, [pp], [p_])
                self.act(q_[:, :], pq[:, :], AF.Copy, [pq], [q_])
                y = yb.get()
                self.tt("vector", y[:, :], p_[:, :], q_[:, :], ALU.add, [p_, q_], [y])
                ydf = None
                if kb < 16:
                    ydf = yd.get()
                    self.tt("gpsimd", ydf[:, :], p_[:, :], q_[:, :], ALU.subtract, [p_, q_], [ydf])
                return y, ydf

            def tail(y, t0, m, mirrored):
                pt = pstr.get()
                for cc in range(4):
                    self.mmul(pt[:, cc * 128:(cc + 1) * 128], y[:, cc * 128:(cc + 1) * 128], identJ if mirrored else self.ident_b[:, :], True, True,
                              [y, self.ident_b], pt, cc == 0)
                yt_ = yT.get()
                self.cp("vector", yt_[:, :, :], pt[:, :].rearrange("p (a b) -> p a b", a=4), [pt], [yt_])
                x_r = xr.get()
                P.dma("sync", x_r[0:m, :], rsrc.ap()[t0:t0 + m, :], reads=[rsrc], writes=[x_r], sembuf=x_r)
                o = ot.get()
                for dh in range(2):
                    po = pso.get()
                    for kc in range(4):
                        self.mmul(po[0:m, :], yt_[:, kc, 0:m], wf[:, kc, dh * 512:(dh + 1) * 512], kc == 0, kc == 3, [yt_, wf], po, kc == 0)
                    t = yt.get()
                    self.tt("vector", t[0:m, :], po[0:m, :], G[0:m, dh * 512:(dh + 1) * 512], ALU.mult, [po, G], [t])
                    self.tt("gpsimd", o[0:m, dh * 512:(dh + 1) * 512], t[0:m, :], x_r[0:m, dh * 512:(dh + 1) * 512], ALU.add, [t, x_r],
                            writes=[o] if dh == 0 else [], awrites=[] if dh == 0 else [o])
                P.dma(STQ, dst.ap()[t0:t0 + m, :], o[0:m, :], reads=[o], awrites=[dst], sembuf=o)

            def tails(kb, y, ydf):
                if kb < 16:
                    tail(y, kb * 128, 128, False)
                    tail(ydf, 128 * (31 - kb) + 1, 127 if kb == 0 else 128, True)
                else:
                    tail(y, 2048, 1, False)

            prev = None
            for kb in range(17):
                y, ydf = head(kb)
                if prev is not None:
                    tails(*prev)
                prev = (kb, y, ydf)
            tails(*prev)
            P.pop()

    def build(self):
        self.declare()
        self.setup()
        if self.start_at != "l1":
            self.convert_ffn(0, 0)
        else:
            self.convert_ffn(1, 0)
        self.ada_phase()
        if self.start_at == "l1":
            self.convert_ffn(1, 1)
        cur = self.x
        if self.start_at != "l1":
            self.convert_ffn(0, 1)
            self.ffn_phase("f0", 0, 0, 0, [(self.x, self.xs[0], NT, 0), (self.ctx, self.ctxs, NCTX, 1)])
            self.convert_ffn(1, 0)
            self.convert_ffn(1, 1)
            if self.stop_after == "ffn0":
                return self.finish([self.xs[0], self.ctxs])
            self.mixer_phase("mx", self.xs[0], self.ctxs, self.xs[1])
            if self.stop_after == "mix":
                return self.finish([self.xs[1]])
            self.ffn_phase("f1", 0, 1, 2, [(self.xs[1], self.xs[2], NT, 0)])
            if self.stop_after == "l0":
                return self.finish([self.xs[2]])
            cur = self.xs[2]
        self.ffn_phase("f2", 1, 0, 0, [(cur, self.xs[3], NT, 0)])
        if self.stop_after == "ffn2":
            return self.finish([self.xs[3]])
        self.fourier_phase("fo", 1, 1, self.xs[3], self.xs[4])
        if self.stop_after == "four":
            return self.finish([self.xs[4]])
        self.ffn_phase("f3", 1, 1, 2, [(self.xs[4], self.out, NT, 0)])
        return self.finish([self.out])

    def finish(self, bufs):
        P = self.P
        if self.stop_after is not None:
            for i, b in enumerate(bufs):
                shape = [int(s) for s in b.ap().shape]
                o = self.dout("dbg%d" % i, shape)
                P.dma("sync", o.ap(), b.ap(), reads=[b], writes=[o], sembuf=o)
        P.final_wait("sync", list(self.outputs.values()))
        P.emit()
        return self.nc


def host_constants(need_dft=True):
    c = {}
    c["c_ident"] = np.eye(128, dtype=np.float32)
    rows = NT // GRID_W
    row = np.repeat(np.arange(rows, dtype=np.float32), GRID_W)
    col = np.tile(np.arange(GRID_W, dtype=np.float32), rows)
    n_freq = HD // 4
    inv_freq = (1.0 / (10000.0 ** (np.arange(n_freq, dtype=np.float32) / n_freq))).astype(np.float32)
    ang = np.concatenate([row[:, None] * inv_freq, col[:, None] * inv_freq], axis=-1).astype(np.float32)
    c["c_rope"] = np.concatenate([np.cos(ang), np.sin(ang)], axis=-1).astype(np.float32)
    i = np.arange(128)[:, None]
    r = np.arange(128)[None, :]
    c["c_mask"] = np.stack([(i >= r), (i <= r)]).astype(np.float32)
    c["c_iota"] = np.arange(1, 513, dtype=np.float32)[None, :]
    k = np.arange(256, dtype=np.float64)
    a = 2.0 * np.pi * np.outer(k, k) / 256.0
    c["c_dftc"] = (np.stack([np.cos(a), np.sin(a)]) / 16.0).astype(np.float32)
    if not need_dft:
        return c
    n = np.arange(NT, dtype=np.int64)
    nk = (np.outer(n, n) % NT).astype(np.float64)
    an = 2.0 * np.pi * nk / NT
    import ml_dtypes
    t = (np.stack([np.cos(an), -np.sin(an)]) / 64.0).astype(np.float32).astype(ml_dtypes.bfloat16)
    t = t.reshape(2, 32, 128, 32, 128)
    c["c_dftn"] = np.ascontiguousarray(t.transpose(0, 3, 2, 1, 4))
    return c


_CACHE = {}


def make_in_maps(inputs, b_list, need_dft=True):
    consts = _CACHE.get(("consts", need_dft))
    if consts is None:
        consts = host_constants(need_dft)
        _CACHE[("consts", need_dft)] = consts
    f = lambda a: np.ascontiguousarray(np.asarray(a, dtype=np.float32))
    shared = {
        "w_ada": f(inputs["w_ada"]), "b_ada": f(inputs["b_ada"]), "norm_gain": f(inputs["norm_gain"]),
        "ffn_w1": f(inputs["ffn_w1"]), "ffn_w3": f(inputs["ffn_w3"]), "ffn_w2": f(inputs["ffn_w2"]),
        "w_in": f(inputs["w_in"][0]), "q_gain": f(inputs["q_gain"]), "k_gain": f(inputs["k_gain"]),
        "sink_logit": f(inputs["sink_logit"]),
        "ssm_lam_re": f(inputs["ssm_lam_re"][0]), "ssm_lam_im": f(inputs["ssm_lam_im"][0]), "ssm_log_dt": f(inputs["ssm_log_dt"][0]),
        "ssm_b_re": f(inputs["ssm_b_re"][0]), "ssm_b_im": f(inputs["ssm_b_im"][0]),
        "ssm_c_re": f(inputs["ssm_c_re"][0]), "ssm_c_im": f(inputs["ssm_c_im"][0]),
        "ssm_d": f(inputs["ssm_d"]), "ssm_w_glu": f(inputs["ssm_w_glu"][0]), "w_out": f(inputs["w_out"][0]),
        "fourier_w_out": f(inputs["fourier_w_out"][0]),
    }
    shared.update(consts)
    maps = []
    x = np.asarray(inputs["x"], dtype=np.float32)
    ctx = np.asarray(inputs["ctx"], dtype=np.float32)
    c = np.asarray(inputs["c"], dtype=np.float32)
    c_ctx = np.asarray(inputs["c_ctx"], dtype=np.float32)
    for b in b_list:
        m = dict(shared)
        m["x"] = np.ascontiguousarray(x[b])
        m["ctx"] = np.ascontiguousarray(ctx[b])
        m["cvec"] = np.ascontiguousarray(np.stack([c[b], c_ctx]))
        maps.append(m)
    return maps


def kernel(**inputs):
    k = K()
    nc = k.build()
    in_maps = make_in_maps(inputs, list(range(8)), need_dft=("c_dftn" in k.inputs))
    in_maps = [{n: m[n] for n in k.inputs} for m in in_maps]
    res = run_bass_kernel_spmd(nc, in_maps, core_ids=list(range(8)))
    return np.stack([np.asarray(r["out"], dtype=np.float32) for r in res.results], axis=0)
```
